# Optimizing a Trainium2 kernel written in Bass

```python
import jax, jax.numpy as jnp
from jax import lax
import numpy as np

D_MODEL = 4096
BATCH = 4
SEQ = 2048
DEPTH = 2

HEAD_DIM = 128
ROPE_THETA = 10000.0
NORM_EPS = 1e-6
PLE_DIM = 256
NEG_INF = -1e30

SWA_HEADS = 12
SWA_KV_HEADS = 4
SWA_GROUP = SWA_HEADS // SWA_KV_HEADS
SWA_WINDOW = 128
SWA_BLOCK = 128
SWA_OUT = SWA_HEADS * HEAD_DIM

RG_WIDTH = 1024
RG_BLOCKS = 8
RG_BLOCK_DIM = RG_WIDTH // RG_BLOCKS
RG_CONV = 4
RG_C = 8.0

MLA_HEADS = 12
MLA_Q_RANK = 1024
MLA_KV_RANK = 512
MLA_NOPE = 128
MLA_ROPE = 64
MLA_V = 128
MLA_QBLOCK = 128
MLA_OUT = MLA_HEADS * MLA_V

MIX_WIDTH = SWA_OUT + RG_WIDTH + MLA_OUT

IN_SIZES = (SWA_OUT, SWA_KV_HEADS * HEAD_DIM, SWA_KV_HEADS * HEAD_DIM,
            RG_WIDTH, RG_WIDTH, MLA_Q_RANK, MLA_KV_RANK, MLA_ROPE)
IN_WIDTH = sum(IN_SIZES)
IN_SPLITS = tuple(int(v) for v in np.cumsum(IN_SIZES)[:-1])

D_FF = -(-8 * D_MODEL // (3 * 256)) * 256

kernel_name = 'hybrid_parallel_heads_swa_rglru_mla'


def rmsnorm(x, g):
    xf = x.astype(jnp.float32)
    y = xf * lax.rsqrt(jnp.mean(xf * xf, axis=-1, keepdims=True) + NORM_EPS)
    return (y * g.astype(jnp.float32)).astype(x.dtype)


def rope(x, positions):
    d = x.shape[-1]
    inv = ROPE_THETA ** (-jnp.arange(0, d, 2, dtype=jnp.float32) / d)
    ang = positions.astype(jnp.float32)[:, :, None] * inv
    cos = jnp.cos(ang)[:, :, None, :]
    sin = jnp.sin(ang)[:, :, None, :]
    xf = x.astype(jnp.float32)
    x1, x2 = jnp.split(xf, 2, axis=-1)
    return jnp.concatenate([x1 * cos - x2 * sin, x2 * cos + x1 * sin], axis=-1).astype(x.dtype)


def swa_attention(q, k, v, sinks):
    B, S = q.shape[0], q.shape[1]
    L = SWA_BLOCK
    nb = S // L
    qb = q.reshape(B, nb, L, SWA_KV_HEADS, SWA_GROUP, HEAD_DIM)

    def with_prev(t):
        t = t.reshape(B, nb, L, SWA_KV_HEADS, HEAD_DIM)
        prev = jnp.pad(t[:, :-1], ((0, 0), (1, 0), (0, 0), (0, 0), (0, 0)))
        return jnp.concatenate([prev, t], axis=2)

    kk, vv = with_prev(k), with_prev(v)
    scores = jnp.einsum('bnqkgd,bnskd->bnkgqs', qb, kk,
                        preferred_element_type=jnp.float32) * (HEAD_DIM ** -0.5)
    blk = jnp.arange(nb)[:, None, None]
    qpos = blk * L + jnp.arange(L)[None, :, None]
    kpos = (blk - 1) * L + jnp.arange(2 * L)[None, None, :]
    valid = (kpos >= 0) & (kpos <= qpos) & (qpos - kpos < SWA_WINDOW)
    scores = jnp.where(valid[None, :, None, None], scores, NEG_INF)
    sink = sinks.astype(jnp.float32).reshape(SWA_KV_HEADS, SWA_GROUP)[None, None, :, :, None, None]
    sink = jnp.broadcast_to(sink, scores.shape[:-1] + (1,))
    probs = jax.nn.softmax(jnp.concatenate([scores, sink], axis=-1), axis=-1)[..., :-1]
    out = jnp.einsum('bnkgqs,bnskd->bnqkgd', probs.astype(v.dtype), vv)
    return out.reshape(B, S, SWA_OUT)


def rglru(xr, conv_w, conv_b, wa, ba, wx, bx, lam):
    B, S = xr.shape[0], xr.shape[1]
    xc = lax.conv_general_dilated(xr, conv_w[:, None, :], window_strides=(1,),
                                  padding=[(RG_CONV - 1, 0)],
                                  dimension_numbers=('NWC', 'WIO', 'NWC'),
                                  feature_group_count=RG_WIDTH) + conv_b
    xg = xc.reshape(B, S, RG_BLOCKS, RG_BLOCK_DIM)
    r = jax.nn.sigmoid((jnp.einsum('bsnc,ncd->bsnd', xg, wa).reshape(B, S, RG_WIDTH) + ba).astype(jnp.float32))
    i = jax.nn.sigmoid((jnp.einsum('bsnc,ncd->bsnd', xg, wx).reshape(B, S, RG_WIDTH) + bx).astype(jnp.float32))
    log_a = -RG_C * jax.nn.softplus(-lam.astype(jnp.float32)) * r
    a = jnp.exp(log_a)
    b = jnp.sqrt(-jnp.expm1(2.0 * log_a)) * i * xc.astype(jnp.float32)

    def combine(left, right):
        a1, b1 = left
        a2, b2 = right
        return a1 * a2, a2 * b1 + b2

    _, h = lax.associative_scan(combine, (a, b), axis=1)
    return h.astype(xr.dtype)


def mla_attention(cq, ckv, kr, positions, q_norm, w_uq, kv_norm, w_ukv):
    B, S = cq.shape[0], cq.shape[1]
    q = (rmsnorm(cq, q_norm) @ w_uq).reshape(B, S, MLA_HEADS, MLA_NOPE + MLA_ROPE)
    q_nope = q[..., :MLA_NOPE]
    q_rope = rope(q[..., MLA_NOPE:], positions)
    kv = (rmsnorm(ckv, kv_norm) @ w_ukv).reshape(B, S, MLA_HEADS, MLA_NOPE + MLA_V)
    k_nope, v = kv[..., :MLA_NOPE], kv[..., MLA_NOPE:]
    k_rope = rope(kr[:, :, None, :], positions)[:, :, 0]
    scale = (MLA_NOPE + MLA_ROPE) ** -0.5
    outs = []
    for j in range(S // MLA_QBLOCK):
        q0 = j * MLA_QBLOCK
        end = q0 + MLA_QBLOCK
        s = (jnp.einsum('bqhd,bkhd->bhqk', q_nope[:, q0:end], k_nope[:, :end],
                        preferred_element_type=jnp.float32)
             + jnp.einsum('bqhd,bkd->bhqk', q_rope[:, q0:end], k_rope[:, :end],
                          preferred_element_type=jnp.float32)) * scale
        causal = jnp.arange(end)[None, :] <= (q0 + jnp.arange(MLA_QBLOCK))[:, None]
        s = jnp.where(causal, s, NEG_INF)
        pr = jax.nn.softmax(s, axis=-1).astype(v.dtype)
        outs.append(jnp.einsum('bhqk,bkhd->bqhd', pr, v[:, :end]))
    return jnp.concatenate(outs, axis=1).reshape(B, S, MLA_OUT)


def setup_inputs(seed: int = 0) -> dict:
    key = jax.random.key(seed)
    ks = iter(jax.random.split(key, 40))

    def nrm(shape, scale):
        return jax.random.normal(next(ks), shape, jnp.float32) * scale

    def gain(n):
        return 1.0 + 0.05 * nrm((DEPTH, n), 1.0)

    x = nrm((BATCH, SEQ, D_MODEL), 1.0)
    p = nrm((DEPTH, BATCH, SEQ, PLE_DIM), 1.0)
    offset = jax.random.randint(next(ks), (BATCH,), 0, 4096, dtype=jnp.int32)
    positions = offset[:, None] + jnp.arange(SEQ, dtype=jnp.int32)[None, :]
    u = jax.random.uniform(next(ks), (DEPTH, RG_WIDTH), jnp.float32, 0.9, 0.999)
    a_base = u ** (1.0 / RG_C)
    rg_lambda = jnp.log(a_base) - jnp.log1p(-a_base)
    return {
        'x': x,
        'p': p,
        'positions': positions,
        'pre_mix_norm': gain(D_MODEL),
        'w_in': nrm((DEPTH, D_MODEL, IN_WIDTH), D_MODEL ** -0.5),
        'swa_sinks': nrm((DEPTH, SWA_HEADS), 1.0),
        'rg_conv_w': nrm((DEPTH, RG_CONV, RG_WIDTH), RG_CONV ** -0.5),
        'rg_conv_b': nrm((DEPTH, RG_WIDTH), 0.02),
        'rg_gate_a_w': nrm((DEPTH, RG_BLOCKS, RG_BLOCK_DIM, RG_BLOCK_DIM), RG_BLOCK_DIM ** -0.5),
        'rg_gate_a_b': nrm((DEPTH, RG_WIDTH), 0.1),
        'rg_gate_x_w': nrm((DEPTH, RG_BLOCKS, RG_BLOCK_DIM, RG_BLOCK_DIM), RG_BLOCK_DIM ** -0.5),
        'rg_gate_x_b': nrm((DEPTH, RG_WIDTH), 0.1),
        'rg_lambda': rg_lambda,
        'mla_q_norm': gain(MLA_Q_RANK),
        'mla_w_uq': nrm((DEPTH, MLA_Q_RANK, MLA_HEADS * (MLA_NOPE + MLA_ROPE)), MLA_Q_RANK ** -0.5),
        'mla_kv_norm': gain(MLA_KV_RANK),
        'mla_w_ukv': nrm((DEPTH, MLA_KV_RANK, MLA_HEADS * (MLA_NOPE + MLA_V)), MLA_KV_RANK ** -0.5),
        'group_norm': gain(MIX_WIDTH),
        'w_out': nrm((DEPTH, MIX_WIDTH, D_MODEL), MIX_WIDTH ** -0.5),
        'post_mix_norm': gain(D_MODEL),
        'pre_ffn_norm': gain(D_MODEL),
        'w_gate': nrm((DEPTH, D_MODEL, D_FF), D_MODEL ** -0.5),
        'w_up': nrm((DEPTH, D_MODEL, D_FF), D_MODEL ** -0.5),
        'w_down': nrm((DEPTH, D_FF, D_MODEL), D_FF ** -0.5),
        'post_ffn_norm': gain(D_MODEL),
        'w_ple': nrm((DEPTH, PLE_DIM, D_MODEL), PLE_DIM ** -0.5),
        'ple_norm': gain(D_MODEL),
        'w_ple_gate': nrm((DEPTH, D_MODEL, D_MODEL), D_MODEL ** -0.5),
        'b_ple_gate': nrm((DEPTH, D_MODEL), 0.1),
    }


def reference(x, p, positions, pre_mix_norm, w_in, swa_sinks, rg_conv_w, rg_conv_b,
              rg_gate_a_w, rg_gate_a_b, rg_gate_x_w, rg_gate_x_b, rg_lambda,
              mla_q_norm, mla_w_uq, mla_kv_norm, mla_w_ukv, group_norm, w_out,
              post_mix_norm, pre_ffn_norm, w_gate, w_up, w_down, post_ffn_norm,
              w_ple, ple_norm, w_ple_gate, b_ple_gate):
    B, S = x.shape[0], x.shape[1]
    for i in range(DEPTH):
        h = rmsnorm(x, pre_mix_norm[i])
        z = h @ w_in[i]
        q_a, k_a, v_a, x_r, g_r, c_q, c_kv, k_r = jnp.split(z, IN_SPLITS, axis=-1)
        q_a = rope(q_a.reshape(B, S, SWA_HEADS, HEAD_DIM), positions)
        k_a = rope(k_a.reshape(B, S, SWA_KV_HEADS, HEAD_DIM), positions)
        v_a = v_a.reshape(B, S, SWA_KV_HEADS, HEAD_DIM)
        o_a = swa_attention(q_a, k_a, v_a, swa_sinks[i])
        o_b = rglru(x_r, rg_conv_w[i], rg_conv_b[i], rg_gate_a_w[i], rg_gate_a_b[i],
                    rg_gate_x_w[i], rg_gate_x_b[i], rg_lambda[i]) * jax.nn.gelu(g_r)
        o_c = mla_attention(c_q, c_kv, k_r, positions, mla_q_norm[i], mla_w_uq[i],
                            mla_kv_norm[i], mla_w_ukv[i])
        gn = group_norm[i]
        mixed = jnp.concatenate([
            rmsnorm(o_a, gn[:SWA_OUT]),
            rmsnorm(o_b, gn[SWA_OUT:SWA_OUT + RG_WIDTH]),
            rmsnorm(o_c, gn[SWA_OUT + RG_WIDTH:]),
        ], axis=-1)
        x = x + rmsnorm(mixed @ w_out[i], post_mix_norm[i])
        h = rmsnorm(x, pre_ffn_norm[i])
        f = (jax.nn.silu(h @ w_gate[i]) * (h @ w_up[i])) @ w_down[i]
        x = x + rmsnorm(f, post_ffn_norm[i])
        e = rmsnorm(p[i] @ w_ple[i], ple_norm[i])
        x = x + jax.nn.sigmoid(x @ w_ple_gate[i] + b_ple_gate[i]) * e
    return x
```

```python
import math
from contextlib import ExitStack
import numpy as np
import concourse.bass as bass
import concourse.mybir as mybir
from concourse.bass_utils import run_bass_kernel_spmd

dt = mybir.dt
F32, BF16, I32 = dt.float32, dt.bfloat16, dt.int32
AF = mybir.ActivationFunctionType
ALU = mybir.AluOpType
EPS = 1e-6
THETA = 10000.0


class Cfg:
    def __init__(s, D=4096, NT=2048, DFF=11008, HS=12, KV=4, RGW=1024, MH=12, QR=1024, KVR=512, PLE=256, DEPTH=2):
        s.D, s.NT, s.DFF, s.HS, s.KV, s.RGW, s.MH, s.QR, s.KVR, s.PLE, s.DEPTH = D, NT, DFF, HS, KV, RGW, MH, QR, KVR, PLE, DEPTH
        s.DC = D // 128
        s.FC = DFF // 128
        s.RC = RGW // 128
        s.QC = QR // 128
        s.KVC = KVR // 128
        s.PC = PLE // 128
        s.NB = NT // 128
        s.G = HS // KV
        s.INW = HS * 128 + 2 * KV * 128 + 2 * RGW + QR + KVR + 64
        assert HS * 128 + RGW + MH * 128 == D
        o = 0
        s.o_qa = o; o += HS * 128
        s.o_ka = o; o += KV * 128
        s.o_va = o; o += KV * 128
        s.o_xr = o; o += RGW
        s.o_gr = o; o += RGW
        s.o_cq = o; o += QR
        s.o_ckv = o; o += KVR
        s.o_kr = o; o += 64
        names = [("pre_mix", s.DC), ("cw0", s.RC), ("cw1", s.RC), ("cw2", s.RC), ("cw3", s.RC), ("cb", s.RC),
                 ("ba", s.RC), ("bx", s.RC), ("lam", s.RC), ("qn", s.QC), ("kvn", s.KVC), ("gn", s.DC),
                 ("post_mix", s.DC), ("pre_ffn", s.DC), ("post_ffn", s.DC), ("ple_n", s.DC), ("bpg", s.DC)]
        s.voff = {}
        o = 0
        for n, c in names:
            s.voff[n] = o
            o += c
        s.NV = o
        s.c_ones, s.c_id, s.c_rswa, s.c_rmla = 0, 128, 256, 384
        s.c_mask, s.c_mask0, s.c_negtri = 512, 1024, 1536
        s.c_invs, s.c_invm = 1664, 1665
        s.CW = 1666


class Buf:
    __slots__ = ("w", "r", "excl")

    def __init__(s, excl=False):
        s.w = {}
        s.r = {}
        s.excl = excl


class Sched:
    def __init__(s, nc, es, ndma=8):
        s.nc = nc
        s.eng = {"pe": nc.tensor, "act": nc.scalar, "dve": nc.vector, "pool": nc.gpsimd, "sp": nc.sync}
        s.sems = []
        s.cs = {}
        s.cnt = {}
        for e in ("pe", "act", "dve", "pool"):
            s.cs[e] = len(s.sems)
            s.sems.append(es.enter_context(nc.semaphore("c_" + e)))
            s.cnt[e] = 0
        s.dq = {}
        s.dqn = {}
        s.first_dma = len(s.sems)
        for q in ("sp", "pool"):
            s.dq[q] = []
            for i in range(ndma):
                s.dq[q].append([len(s.sems), 0])
                s.sems.append(es.enter_context(nc.semaphore("d_%s%d" % (q, i))))
            s.dqn[q] = 0
        s.waited = {e: {} for e in s.eng}
        s.pe_open = False
        s.nwait = 0

    def _wait(s, e, si, val):
        if s.waited[e].get(si, 0) >= val:
            return
        s.eng[e].wait_ge(s.sems[si], val)
        s.waited[e][si] = val
        s.nwait += 1

    def _deps(s, e, reads, writes, is_dma=False):
        need = {}
        for b in reads:
            for si, v in b.w.items():
                if need.get(si, 0) < v:
                    need[si] = v
            if b.excl:
                for si, v in b.r.items():
                    if need.get(si, 0) < v:
                        need[si] = v
        for b in writes:
            for si, v in b.w.items():
                if is_dma and si >= s.first_dma:
                    continue
                if need.get(si, 0) < v:
                    need[si] = v
            for si, v in b.r.items():
                if need.get(si, 0) < v:
                    need[si] = v
        for si, v in need.items():
            if e == "pe" and si == s.cs["pe"]:
                continue
            s._wait(e, si, v)

    def _mark(s, ev, reads, writes, is_dma=False):
        for b in writes:
            if is_dma:
                b.w[ev[0]] = ev[1]
            else:
                b.w = {ev[0]: ev[1]}
            b.r = {}
        for b in reads:
            if b.r.get(ev[0], 0) < ev[1]:
                b.r[ev[0]] = ev[1]

    def op(s, e, fn, reads=(), writes=(), last=True):
        s._deps(e, reads, writes)
        ins = fn(s.eng[e])
        if e == "pe" and not last:
            ev = (s.cs[e], s.cnt[e] + 1)
        else:
            s.cnt[e] += 1
            ins.then_inc(s.sems[s.cs[e]], 1)
            ev = (s.cs[e], s.cnt[e])
        s._mark(ev, reads, writes)
        return ins

    def dma(s, q, out, in_, reads=(), writes=(), **kw):
        slots = s.dq[q]
        slot = slots[s.dqn[q] % len(slots)]
        s.dqn[q] += 1
        if slot[1] > 0:
            s._wait(q, slot[0], slot[1])
        s._deps(q, reads, writes, is_dma=True)
        ins = s.eng[q].dma_start(out=out, in_=in_, **kw)
        slot[1] += 16
        ins.then_inc(s.sems[slot[0]], 16)
        s._mark((slot[0], slot[1]), reads, writes, is_dma=True)

    def barrier(s, final=False):
        for e in s.eng:
            for ce in ("pe", "act", "dve", "pool"):
                if s.cnt[ce] > 0:
                    s._wait(e, s.cs[ce], s.cnt[ce])
            for q in s.dq:
                if q == "pool" and not final:
                    continue
                for si, v in s.dq[q]:
                    if v > 0:
                        s._wait(e, si, v)


def build(cfg, dbg=False, phases=None):
    c = cfg
    nc = bass.Bass("TRN2", target_bir_lowering=False)
    D, NT, DC, FC, RC, QC, KVC, PC, NB, HS, KV, MH, L = c.D, c.NT, c.DC, c.FC, c.RC, c.QC, c.KVC, c.PC, c.NB, c.HS, c.KV, c.MH, c.DEPTH
    T1 = min(512, NT)
    T2 = min(512, NT)
    GW = 256

    def din(name, shape, d=F32):
        return nc.dram_tensor(name, list(shape), d, kind="ExternalInput").ap()

    def dscr(name, shape, d=F32):
        kind = "ExternalOutput" if (dbg and name in dbg) else "Internal"
        return nc.dram_tensor(name, list(shape), d, kind=kind).ap()

    xT = din("xT", [D, NT])
    pT = din("pT", [L, c.PLE, NT])
    posb = din("posb", [128, NT], I32)
    vecs = din("vecs", [L, 128, c.NV])
    sinkb = din("sinkb", [L, 128, HS])
    consts = din("consts", [128, c.CW])
    w_in = din("w_in", [L, D, c.INW])
    w_ga = din("w_ga", [L, RC, 128, 128])
    w_gx = din("w_gx", [L, RC, 128, 128])
    w_uq = din("w_uq", [L, c.QR, MH * 192])
    w_ukv = din("w_ukv", [L, c.KVR, MH * 256])
    w_out = din("w_out", [L, D, D])
    w_gate = din("w_gate", [L, D, c.DFF])
    w_up = din("w_up", [L, D, c.DFF])
    w_down = din("w_down", [L, c.DFF, D])
    w_ple = din("w_ple", [L, c.PLE, D])
    w_pg = din("w_pg", [L, D, D])
    yT = nc.dram_tensor("yT", [D, NT], F32, kind="ExternalOutput").ap()

    NG_IN = (c.INW - 64) // GW
    assert (c.INW - 64) % GW == 0
    wb_in = dscr("wb_in", [L, NG_IN, 128, DC, GW], BF16)
    wb_kr = dscr("wb_kr", [L, 128, DC, 128], BF16)
    wb_ga = dscr("wb_ga", [L, 128, RC, 128], BF16)
    wb_gx = dscr("wb_gx", [L, 128, RC, 128], BF16)
    wb_uqn = dscr("wb_uqn", [L, 128, QC, MH * 128], BF16)
    wb_uqr = dscr("wb_uqr", [L, 128, QC, MH * 64], BF16)
    wb_ukk = dscr("wb_ukk", [L, 128, KVC, MH * 128], BF16)
    wb_ukv = dscr("wb_ukv", [L, 128, KVC, MH * 128], BF16)
    wb_out = dscr("wb_out", [L, DC, 128, DC, 128], BF16)
    wb_gate = dscr("wb_gate", [L, FC, 128, DC, 128], BF16)
    wb_up = dscr("wb_up", [L, FC, 128, DC, 128], BF16)
    FH = FC // 2
    assert FC % 2 == 0
    wb_down = dscr("wb_down", [L, DC, 2, 128, FH, 128], BF16)
    wb_ple = dscr("wb_ple", [L, DC, 128, PC, 128], BF16)
    wb_pg = dscr("wb_pg", [L, DC, 128, DC, 128], BF16)
    xres = dscr("xres", [D, NT])
    xs1 = dscr("xs1", [D, NT])
    xs2 = dscr("xs2", [D, NT])
    tabs = dscr("tabs", [4, 128, NT])
    s_qa = dscr("s_qa", [HS, 128, NT], BF16)
    s_ka = dscr("s_ka", [KV, 128, NT], BF16)
    s_va = dscr("s_va", [NT, KV * 128], BF16)
    s_xr = dscr("s_xr", [RC, 128, NT])
    s_gr = dscr("s_gr", [RC, 128, NT])
    s_cq = dscr("s_cq", [QC, 128, NT])
    s_ckv = dscr("s_ckv", [KVC, 128, NT])
    s_kr = dscr("s_kr", [128, NT], BF16)
    s_mix = dscr("s_mix", [DC, 128, NT])
    s_qn = dscr("s_qn", [MH, 128, NT], BF16)
    s_qrp = dscr("s_qrp", [MH // 2, 128, NT], BF16)
    s_kn = dscr("s_kn", [MH, 128, NT], BF16)
    s_vc = dscr("s_vc", [NT, MH * 128], BF16)

    es = ExitStack()
    with es:
        S = Sched(nc, es)
        bufs = {}

        def B(*key):
            if key not in bufs:
                bufs[key] = Buf()
            return bufs[key]

        uniq = [0]

        def sb(es_, name, shape, d):
            uniq[0] += 1
            t = es_.enter_context(nc.sbuf_tensor("%s_%d" % (name, uniq[0]), list(shape), d))
            return t

        psum = []
        for i in range(8):
            t = es.enter_context(nc.psum_tensor("ps%d" % i, [128, 512], F32))
            psum.append((t, Buf(excl=True)))
        psn = [0]

        def PS(lo=0, n=6):
            r = psum[lo + psn[0] % n]
            psn[0] += 1
            return r

        cf = sb(es, "cf", [128, c.CW], F32)
        cb = sb(es, "cb", [128, c.CW], BF16)
        vt = sb(es, "vt", [128, L, c.NV], F32)
        c1t = sb(es, "c1t", [128, L, RC], F32)
        skt = sb(es, "skt", [128, L, HS], F32)
        b_cf, b_cb, b_vt, b_c1, b_sk = Buf(), Buf(), Buf(), Buf(), Buf()
        S.dma("sp", cf[:], consts[:, :], writes=[b_cf])
        S.op("dve", lambda e: e.tensor_copy(out=cb[:], in_=cf[:]), reads=[b_cf], writes=[b_cb])
        for l in range(L):
            S.dma("sp", vt[:, l, :], vecs[l], writes=[b_vt])
            S.dma("sp", skt[:, l, :], sinkb[l], writes=[b_sk])
        S.op("act", lambda e: e.activation(out=skt[:], in_=skt[:], func=AF.Exp), reads=[b_sk], writes=[b_sk])
        for l in range(L):
            lo = c.voff["lam"]
            S.op("act", lambda e: e.activation(out=c1t[:, l, :], in_=vt[:, l, lo:lo + RC], func=AF.Exp, scale=-1.0), reads=[b_vt], writes=[b_c1])
        S.op("act", lambda e: e.activation(out=c1t[:], in_=c1t[:], func=AF.Ln, bias=1.0), reads=[b_c1], writes=[b_c1])
        S.op("dve", lambda e: e.tensor_scalar(out=c1t[:], in0=c1t[:], scalar1=-8.0, scalar2=None, op0=ALU.mult), reads=[b_c1], writes=[b_c1])
        ones_b = cb[:, c.c_ones:c.c_ones + 128]
        id_b = cb[:, c.c_id:c.c_id + 128]

        def vcol(l, name, j):
            o = c.voff[name] + j
            return vt[:, l, o:o + 1]

        def cast(dst, src, wkey):
            S.dma("pool", dst, src, writes=[B(*wkey)], max_dma_last_dim=4096)

        def cast_layer_mixer(l):
            for g in range(NG_IN):
                cast(wb_in[l, g], w_in[l][:, g * GW:(g + 1) * GW].rearrange("(kc p) n -> p kc n", p=128), ("w_in", l, g))
            for h in range(2):
                cast(wb_kr[l][:, :, h * 64:(h + 1) * 64], w_in[l][:, c.o_kr:c.o_kr + 64].rearrange("(kc p) n -> p kc n", p=128), ("w_kr", l))
            cast(wb_ga[l], w_ga[l].rearrange("n c d -> c n d"), ("w_g", l))
            cast(wb_gx[l], w_gx[l].rearrange("n c d -> c n d"), ("w_g", l))
            uq = w_uq[l].rearrange("(kc p) (h c) -> p kc h c", p=128, c=192)
            for kc in range(QC):
                cast(wb_uqn[l][:, kc, :].rearrange("p (h c) -> p h c", c=128), uq[:, kc, :, 0:128], ("w_uq", l))
                cast(wb_uqr[l][:, kc, :].rearrange("p (h c) -> p h c", c=64), uq[:, kc, :, 128:192], ("w_uq", l))
            ukv = w_ukv[l].rearrange("(kc p) (h c) -> p kc h c", p=128, c=256)
            for kc in range(KVC):
                cast(wb_ukk[l][:, kc, :].rearrange("p (h c) -> p h c", c=128), ukv[:, kc, :, 0:128], ("w_ukv", l))
                cast(wb_ukv[l][:, kc, :].rearrange("p (h c) -> p h c", c=128), ukv[:, kc, :, 128:256], ("w_ukv", l))

        def cast_layer_rows(l):
            for g in range(DC):
                cast(wb_out[l, g], w_out[l][:, g * 128:(g + 1) * 128].rearrange("(kc p) n -> p kc n", p=128), ("w_out", l, g))
            for f in range(FC):
                cast(wb_gate[l, f], w_gate[l][:, f * 128:(f + 1) * 128].rearrange("(kc p) n -> p kc n", p=128), ("w_gate", l, f))
                cast(wb_up[l, f], w_up[l][:, f * 128:(f + 1) * 128].rearrange("(kc p) n -> p kc n", p=128), ("w_up", l, f))
            for oc in range(DC):
                for h in range(2):
                    cast(wb_down[l, oc, h], w_down[l][h * FH * 128:(h + 1) * FH * 128, oc * 128:(oc + 1) * 128].rearrange("(kc p) n -> p kc n", p=128), ("w_down", l, oc, h))
            for g in range(DC):
                cast(wb_ple[l, g], w_ple[l][:, g * 128:(g + 1) * 128].rearrange("(kc p) n -> p kc n", p=128), ("w_ple", l, g))
            for g in range(DC):
                cast(wb_pg[l, g], w_pg[l][:, g * 128:(g + 1) * 128].rearrange("(kc p) n -> p kc n", p=128), ("w_pg", l, g))

        def pipeline(items, PF):
            n = len(items)
            hs = {}
            for i in range(min(PF, n)):
                hs[i] = items[i][0]()
            for i in range(n):
                if i + PF < n:
                    hs[i + PF] = items[i + PF][0]()
                items[i][1](hs.pop(i))

        def ssq_rstd(ph, name, nfeat):
            pst, psb = psum[7]
            T = ph["T"]
            rst = ph[name]
            rsb = ph[name + "_b"]

            def acc(sq_ap, sq_buf, first, last):
                S.op("pe", lambda e: e.matmul(pst[:, 0:T], lhsT=ones_b, rhs=sq_ap, start=first, stop=last),
                     reads=[sq_buf, b_cb], writes=[psb], last=True)

            def fin():
                S.op("act", lambda e: e.activation(out=rst[:, 0:T], in_=pst[:, 0:T], func=AF.Sqrt, scale=1.0 / nfeat, bias=EPS),
                     reads=[psb], writes=[rsb])
                S.op("dve", lambda e: e.reciprocal(out=rst[:, 0:T], in_=rst[:, 0:T]), reads=[rsb], writes=[rsb])
            return acc, fin

        sqn = [0]

        def square_to(ph, in_ap, in_bufs, T):
            i = sqn[0] % 3
            sqn[0] += 1
            t, b = ph["sq"][i]
            S.op("act", lambda e: e.activation(out=t[:, 0:T], in_=in_ap, func=AF.Square), reads=in_bufs, writes=[b])
            return t[:, 0:T], b

        def setup_tables():
            with ExitStack() as ph:
                posi = sb(ph, "posi", [128, NT], I32)
                posf = sb(ph, "posf", [128, NT], F32)
                ang = sb(ph, "ang", [128, NT], F32)
                t1 = sb(ph, "tb1", [128, NT], F32)
                t2 = sb(ph, "tb2", [128, NT], F32)
                bp, bf_, ba_, b1, b2 = Buf(), Buf(), Buf(), Buf(), Buf()
                S.dma("sp", posi[:], posb[:, :], writes=[bp])
                S.op("dve", lambda e: e.tensor_copy(out=posf[:], in_=posi[:]), reads=[bp], writes=[bf_])
                MAGIC = 12582912.0
                TWO_PI = 2.0 * math.pi
                for ti, (ccol, shift) in enumerate([(c.c_invs, math.pi / 2), (c.c_invs, 0.0), (c.c_invm, math.pi / 2), (c.c_invm, 0.0)]):
                    S.op("dve", lambda e: e.tensor_scalar(out=ang[:], in0=posf[:], scalar1=cf[:, ccol:ccol + 1], scalar2=shift, op0=ALU.mult, op1=ALU.add),
                         reads=[bf_, b_cf], writes=[ba_])
                    S.op("dve", lambda e: e.tensor_scalar(out=t1[:], in0=ang[:], scalar1=1.0 / TWO_PI, scalar2=MAGIC, op0=ALU.mult, op1=ALU.add),
                         reads=[ba_], writes=[b1])
                    S.op("dve", lambda e: e.tensor_scalar(out=t1[:], in0=t1[:], scalar1=MAGIC, scalar2=None, op0=ALU.subtract),
                         reads=[b1], writes=[b1])
                    S.op("dve", lambda e: e.scalar_tensor_tensor(out=t2[:], in0=t1[:], scalar=-TWO_PI, in1=ang[:], op0=ALU.mult, op1=ALU.add),
                         reads=[b1, ba_], writes=[b2])
                    S.op("dve", lambda e: e.tensor_scalar(out=t2[:], in0=t2[:], scalar1=-3.1415925, scalar2=3.1415925, op0=ALU.max, op1=ALU.min),
                         reads=[b2], writes=[b2])
                    S.op("act", lambda e: e.activation(out=t2[:], in_=t2[:], func=AF.Sin), reads=[b2], writes=[b2])
                    S.dma("sp", tabs[ti], t2[:], reads=[b2], writes=[B("tabs")])
                S.barrier()

        def phase_p1(l, xcur, xck):
            with ExitStack() as ph_:
                T = T1
                ph = {"T": T}
                xt = sb(ph_, "p1x", [128, DC, T], F32)
                hT = sb(ph_, "p1h", [128, DC, T], BF16)
                ph["rs"] = sb(ph_, "p1rs", [128, T], F32)
                ph["rs_b"] = Buf()
                ph["sq"] = [(sb(ph_, "p1sq%d" % i, [128, T], BF16), Buf()) for i in range(3)]
                NWB = 3
                wts = [(sb(ph_, "p1w%d" % i, [128, DC, GW], BF16), Buf()) for i in range(NWB)]
                cs_t = sb(ph_, "p1cs", [128, 4, T], F32)
                b_cs = Buf()
                stg = [(sb(ph_, "p1st%d" % i, [128, T], F32), Buf()) for i in range(4)]
                stb = [(sb(ph_, "p1sb%d" % i, [128, 512], BF16), Buf()) for i in range(4)]
                zb = [(sb(ph_, "p1zb%d" % i, [128, T], BF16), Buf()) for i in range(2)]
                tmp = [(sb(ph_, "p1tm%d" % i, [128, T], F32), Buf()) for i in range(4)]
                b_x, b_h = Buf(), Buf()
                cnt = {"w": 0, "st": 0, "sb": 0, "zb": 0, "tm": 0}

                def rot(lst, k):
                    r = lst[cnt[k] % len(lst)]
                    cnt[k] += 1
                    return r

                for tt in range(NT // T):
                    tok = slice(tt * T, (tt + 1) * T)
                    for ti in range(4):
                        S.dma("sp", cs_t[:, ti, :], tabs[ti][:, tok], reads=[B("tabs")], writes=[b_cs])
                    CH = 8 if DC >= 8 else DC
                    for k0 in range(0, DC, CH):
                        S.dma("sp", xt[:, k0:k0 + CH, :], xcur[k0 * 128:(k0 + CH) * 128, tok].rearrange("(kc p) t -> p kc t", p=128),
                              reads=[B(xck)], writes=[b_x])
                    acc, fin = ssq_rstd(ph, "rs", D)
                    for kc in range(DC):
                        sq_ap, sq_b = square_to(ph, xt[:, kc, :], [b_x], T)
                        acc(sq_ap, sq_b, kc == 0, kc == DC - 1)
                    fin()
                    for kc in range(DC):
                        S.op("dve", lambda e: e.scalar_tensor_tensor(out=hT[:, kc, :], in0=xt[:, kc, :], scalar=vcol(l, "pre_mix", kc), in1=ph["rs"][:, 0:T],
                                                                    op0=ALU.mult, op1=ALU.mult), reads=[b_x, ph["rs_b"], b_vt], writes=[b_h])

                    dq_ = []

                    def fm_chunk(wt, wb_, col, epi):
                        pst, psb = PS()
                        for kc in range(DC):
                            S.op("pe", lambda e: e.matmul(pst[:, 0:T], lhsT=wt[:, kc, col:col + 128], rhs=hT[:, kc, :], start=(kc == 0), stop=(kc == DC - 1)),
                                 reads=[wb_, b_h], writes=[psb], last=(kc == DC - 1))
                        while dq_:
                            dq_.pop(0)()
                        epi(pst, psb)

                    def epi_copy(dst):
                        def f(pst, psb):
                            st, stb_ = rot(stg, "st")
                            S.op("act", lambda e: e.activation(out=st[:, 0:T], in_=pst[:, 0:T], func=AF.Copy), reads=[psb], writes=[stb_])
                            S.dma("sp", dst[:, tok], st[:, 0:T], reads=[stb_], writes=[B("scr")])
                        return f

                    def epi_gelu(dst):
                        def f(pst, psb):
                            st, stb_ = rot(stg, "st")
                            S.op("act", lambda e: e.activation(out=st[:, 0:T], in_=pst[:, 0:T], func=AF.Gelu_apprx_tanh), reads=[psb], writes=[stb_])
                            S.dma("sp", dst[:, tok], st[:, 0:T], reads=[stb_], writes=[B("scr")])
                        return f

                    def epi_rope(dst, mla):
                        ci, si = (2, 3) if mla else (0, 1)
                        rcol = c.c_rmla if mla else c.c_rswa

                        def f(pst, psb):
                            z, zb_ = rot(zb, "zb")
                            S.op("act", lambda e: e.activation(out=z[:, 0:T], in_=pst[:, 0:T], func=AF.Copy), reads=[psb], writes=[zb_])
                            t1, t1b = rot(tmp, "tm")
                            S.op("dve", lambda e: e.tensor_tensor(out=t1[:, 0:T], in0=pst[:, 0:T], in1=cs_t[:, ci, :], op=ALU.mult), reads=[psb, b_cs], writes=[t1b])

                            def second():
                                ps2, ps2b = psum[6]
                                S.op("pe", lambda e: e.matmul(ps2[:, 0:T], lhsT=cb[:, rcol:rcol + 128], rhs=z[:, 0:T], start=True, stop=True), reads=[zb_, b_cb], writes=[ps2b])
                                t2, t2b = rot(tmp, "tm")
                                S.op("dve", lambda e: e.tensor_tensor(out=t2[:, 0:T], in0=ps2[:, 0:T], in1=cs_t[:, si, :], op=ALU.mult), reads=[ps2b, b_cs], writes=[t2b])
                                o, ob = rot(stb, "sb")
                                S.op("dve", lambda e: e.tensor_tensor(out=o[:, 0:T], in0=t1[:, 0:T], in1=t2[:, 0:T], op=ALU.add), reads=[t1b, t2b], writes=[ob])
                                S.dma("sp", dst[:, tok], o[:, 0:T], reads=[ob], writes=[B("scr")])
                            dq_.append(second)
                        return f

                    def tm_group(wt, wb_, ncols, dst, dcol):
                        for ts in range(T // 128):
                            pst, psb = PS()
                            for kc in range(DC):
                                S.op("pe", lambda e: e.matmul(pst[:, 0:ncols], lhsT=hT[:, kc, ts * 128:(ts + 1) * 128], rhs=wt[:, kc, 0:ncols], start=(kc == 0), stop=(kc == DC - 1)),
                                     reads=[wb_, b_h], writes=[psb], last=(kc == DC - 1))
                            o, ob = rot(stb, "sb")
                            S.op("act", lambda e: e.activation(out=o[:, 0:ncols], in_=pst[:, 0:ncols], func=AF.Copy), reads=[psb], writes=[ob])
                            S.dma("sp", dst[tt * T + ts * 128: tt * T + (ts + 1) * 128, dcol:dcol + ncols], o[:, 0:ncols], reads=[ob], writes=[B("scr")])

                    def seg_of(col):
                        if col < c.o_ka: return ("rope", s_qa, (col - c.o_qa) // 128)
                        if col < c.o_va: return ("rope", s_ka, (col - c.o_ka) // 128)
                        if col < c.o_xr: return ("va", None, col - c.o_va)
                        if col < c.o_gr: return ("copy", s_xr, (col - c.o_xr) // 128)
                        if col < c.o_cq: return ("gelu", s_gr, (col - c.o_gr) // 128)
                        if col < c.o_ckv: return ("copy", s_cq, (col - c.o_cq) // 128)
                        return ("copy", s_ckv, (col - c.o_ckv) // 128)

                    items = []
                    for g in range(NG_IN):
                        def ld(g=g):
                            wt, wb_ = rot(wts, "w")
                            S.dma("sp", wt[:], wb_in[l, g], reads=[B("w_in", l, g)], writes=[wb_])
                            return wt, wb_

                        def cp(h, g=g):
                            wt, wb_ = h
                            kind = seg_of(g * GW)[0]
                            if kind == "va":
                                tm_group(wt, wb_, GW, s_va, g * GW - c.o_va)
                                return
                            for j in range(GW // 128):
                                kind, dst, idx = seg_of(g * GW + j * 128)
                                if kind == "rope":
                                    fm_chunk(wt, wb_, j * 128, epi_rope(dst[idx], False))
                                elif kind == "gelu":
                                    fm_chunk(wt, wb_, j * 128, epi_gelu(dst[idx]))
                                else:
                                    fm_chunk(wt, wb_, j * 128, epi_copy(dst[idx]))
                        items.append((ld, cp))

                    def ld_kr():
                        wt, wb_ = rot(wts, "w")
                        S.dma("sp", wt[:, :, 0:128], wb_kr[l], reads=[B("w_kr", l)], writes=[wb_])
                        return wt, wb_

                    def cp_kr(h):
                        fm_chunk(h[0], h[1], 0, epi_rope(s_kr, True))
                    items.append((ld_kr, cp_kr))
                    pipeline(items, 2)
                    while dq_:
                        dq_.pop(0)()
                S.barrier()

        def phase_swa(l):
            with ExitStack() as ph_:
                kt = sb(ph_, "swk", [128, NT], BF16)
                vtile = sb(ph_, "swv", [128, NB, 128], BF16)
                qts = [(sb(ph_, "swq%d" % i, [128, NT], BF16), Buf()) for i in range(2)]
                ets = [(sb(ph_, "swe%d" % i, [128, 512], BF16), Buf()) for i in range(4)]
                rcs = [(sb(ph_, "swr%d" % i, [128, 256], F32), Buf()) for i in range(2)]
                ots = [(sb(ph_, "swo%d" % i, [128, NT], F32), Buf()) for i in range(2)]
                b_k, b_v = Buf(), Buf()
                scale = 128.0 ** -0.5
                n = {"q": 0, "e": 0, "r": 0, "o": 0}
                for kvh in range(KV):
                    S.dma("sp", kt[:], s_ka[kvh], reads=[B("scr")], writes=[b_k])
                    S.dma("sp", vtile[:], s_va[:, kvh * 128:(kvh + 1) * 128].rearrange("(nb p) d -> p nb d", p=128), reads=[B("scr")], writes=[b_v])
                    for g in range(c.G):
                        h = kvh * c.G + g
                        if h == 0:
                            S.dma("sp", qts[0][0][:], s_qa[0], reads=[B("scr")], writes=[qts[0][1]])
                        qt, qb = qts[h % 2]
                        if h + 1 < HS:
                            S.dma("sp", qts[(h + 1) % 2][0][:], s_qa[h + 1], reads=[B("scr")], writes=[qts[(h + 1) % 2][1]])
                        ot, ob = ots[n["o"] % 2]; n["o"] += 1
                        def sw_scores(jp):
                            pst, psb = PS()
                            for u in range(2):
                                j = jp * 2 + u
                                jprev = max(j - 1, 0)
                                qs = qt[:, j * 128:(j + 1) * 128]
                                S.op("pe", lambda e: e.matmul(pst[:, u * 256:u * 256 + 128], lhsT=kt[:, jprev * 128:(jprev + 1) * 128], rhs=qs, start=True, stop=True),
                                     reads=[b_k, qb], writes=[psb], last=False)
                                S.op("pe", lambda e: e.matmul(pst[:, u * 256 + 128:u * 256 + 256], lhsT=kt[:, j * 128:(j + 1) * 128], rhs=qs, start=True, stop=True),
                                     reads=[b_k, qb], writes=[psb], last=(u == 1))
                            et, eb = ets[n["e"] % 4]; n["e"] += 1
                            S.op("act", lambda e: e.activation(out=et[:], in_=pst[:], func=AF.Exp, scale=scale), reads=[psb], writes=[eb])
                            mcol = c.c_mask0 if jp == 0 else c.c_mask
                            S.op("dve", lambda e: e.tensor_tensor(out=et[:], in0=et[:], in1=cb[:, mcol:mcol + 512], op=ALU.mult), reads=[eb, b_cb], writes=[eb])
                            return et, eb

                        def sw_rest(jp, et, eb):
                            pso, psob = PS()
                            psd, psdb = PS()
                            for u in range(2):
                                j = jp * 2 + u
                                jprev = max(j - 1, 0)
                                S.op("pe", lambda e: e.matmul(pso[:, u * 128:(u + 1) * 128], lhsT=vtile[:, jprev, :], rhs=et[:, u * 256:u * 256 + 128], start=True, stop=False),
                                     reads=[b_v, eb], writes=[psob], last=False)
                                S.op("pe", lambda e: e.matmul(pso[:, u * 128:(u + 1) * 128], lhsT=vtile[:, j, :], rhs=et[:, u * 256 + 128:u * 256 + 256], start=False, stop=True),
                                     reads=[b_v, eb], writes=[psob], last=False)
                            for u in range(2):
                                S.op("pe", lambda e: e.matmul(psd[:, u * 128:(u + 1) * 128], lhsT=ones_b, rhs=et[:, u * 256:u * 256 + 128], start=True, stop=False),
                                     reads=[b_cb, eb], writes=[psdb], last=False)
                                S.op("pe", lambda e: e.matmul(psd[:, u * 128:(u + 1) * 128], lhsT=ones_b, rhs=et[:, u * 256 + 128:u * 256 + 256], start=False, stop=True),
                                     reads=[b_cb, eb], writes=[psdb], last=(u == 1))
                            rc, rb = rcs[n["r"] % 2]; n["r"] += 1
                            S.op("dve", lambda e: e.tensor_scalar(out=rc[:], in0=psd[:, 0:256], scalar1=skt[:, l, h:h + 1], scalar2=None, op0=ALU.add), reads=[psdb, b_sk], writes=[rb])
                            S.op("dve", lambda e: e.reciprocal(out=rc[:], in_=rc[:]), reads=[rb], writes=[rb])
                            S.op("dve", lambda e: e.tensor_tensor(out=ot[:, jp * 256:(jp + 1) * 256], in0=pso[:, 0:256], in1=rc[:], op=ALU.mult), reads=[psob, rb], writes=[ob])

                        pend = [sw_scores(jp) for jp in range(min(2, NB // 2))]
                        for jp in range(NB // 2):
                            if jp + 2 < NB // 2:
                                pend.append(sw_scores(jp + 2))
                            sw_rest(jp, *pend.pop(0))
                        S.dma("sp", s_mix[h], ot[:], reads=[ob], writes=[B("mix")])
                S.barrier()

        def phase_rg(l):
            with ExitStack() as ph_:
                wa = sb(ph_, "rgwa", [128, RC, 128], BF16)
                wx = sb(ph_, "rgwx", [128, RC, 128], BF16)
                b_w = Buf()
                S.dma("sp", wa[:], wb_ga[l], reads=[B("w_g", l)], writes=[b_w])
                S.dma("sp", wx[:], wb_gx[l], reads=[B("w_g", l)], writes=[b_w])
                xp = [(sb(ph_, "rgx%d" % i, [128, NT + 3], F32), Buf()) for i in range(2)]
                gl = [(sb(ph_, "rgg%d" % i, [128, NT], F32), Buf()) for i in range(2)]
                names = ["xc", "r", "i", "a", "t"]
                tl2 = {nm: [(sb(ph_, "rg_%s%d" % (nm, i), [128, NT], F32), Buf()) for i in range(2)] for nm in names}
                xcb2 = [(sb(ph_, "rg_xcb%d" % i, [128, NT], BF16), Buf()) for i in range(2)]
                for i in range(2):
                    S.op("dve", lambda e: e.memset(xp[i][0][:, 0:3], 0.0), writes=[xp[i][1]])
                for n_ in range(RC):
                    tl = {nm: tl2[nm][n_ % 2] for nm in names}
                    xcb = xcb2[n_ % 2]
                    xpt, xpb = xp[n_ % 2]
                    gt, gb = gl[n_ % 2]
                    S.dma("sp", xpt[:, 3:3 + NT], s_xr[n_], reads=[B("scr")], writes=[xpb])
                    S.dma("sp", gt[:], s_gr[n_], reads=[B("scr")], writes=[gb])
                    xc, xcbuf = tl["xc"]
                    S.op("dve", lambda e: e.tensor_scalar(out=xc[:], in0=xpt[:, 3:3 + NT], scalar1=vcol(l, "cw3", n_), scalar2=vcol(l, "cb", n_), op0=ALU.mult, op1=ALU.add),
                         reads=[xpb, b_vt], writes=[xcbuf])
                    for j in range(3):
                        S.op("dve", lambda e: e.scalar_tensor_tensor(out=xc[:], in0=xpt[:, j:j + NT], scalar=vcol(l, "cw%d" % j, n_), in1=xc[:], op0=ALU.mult, op1=ALU.add),
                             reads=[xpb, b_vt, xcbuf], writes=[xcbuf])
                    S.op("act", lambda e: e.activation(out=xcb[0][:], in_=xc[:], func=AF.Copy), reads=[xcbuf], writes=[xcb[1]])
                    rt, rb = tl["r"]
                    it, ib = tl["i"]
                    for tt in range(NT // T1):
                        tok = slice(tt * T1, (tt + 1) * T1)
                        for (wt_, dst, dbuf, bname) in ((wa, rt, rb, "ba"), (wx, it, ib, "bx")):
                            pst, psb = PS()
                            S.op("pe", lambda e: e.matmul(pst[:, 0:T1], lhsT=wt_[:, n_, :], rhs=xcb[0][:, tok], start=True, stop=True), reads=[b_w, xcb[1]], writes=[psb])
                            S.op("act", lambda e: e.activation(out=dst[:, tok], in_=pst[:, 0:T1], func=AF.Sigmoid, bias=vcol(l, bname, n_)), reads=[psb, b_vt], writes=[dbuf])
                    at, ab = tl["a"]
                    tt_, tb = tl["t"]
                    S.op("act", lambda e: e.activation(out=at[:], in_=rt[:], func=AF.Exp, scale=c1t[:, l, n_:n_ + 1]), reads=[rb, b_c1], writes=[ab])
                    S.op("dve", lambda e: e.tensor_tensor(out=tt_[:], in0=at[:], in1=at[:], op=ALU.mult), reads=[ab], writes=[tb])
                    S.op("dve", lambda e: e.tensor_scalar(out=tt_[:], in0=tt_[:], scalar1=-1.0, scalar2=1.0, op0=ALU.mult, op1=ALU.add), reads=[tb], writes=[tb])
                    S.op("dve", lambda e: e.tensor_scalar(out=tt_[:], in0=tt_[:], scalar1=1e-20, scalar2=None, op0=ALU.max), reads=[tb], writes=[tb])
                    S.op("act", lambda e: e.activation(out=tt_[:], in_=tt_[:], func=AF.Sqrt), reads=[tb], writes=[tb])
                    S.op("dve", lambda e: e.tensor_tensor(out=tt_[:], in0=tt_[:], in1=it[:], op=ALU.mult), reads=[tb, ib], writes=[tb])
                    S.op("dve", lambda e: e.tensor_tensor(out=tt_[:], in0=tt_[:], in1=xc[:], op=ALU.mult), reads=[tb, xcbuf], writes=[tb])
                    S.op("dve", lambda e: e.tensor_tensor_scan(out=rt[:], data0=at[:], data1=tt_[:], initial=0.0, op0=ALU.mult, op1=ALU.add), reads=[ab, tb, rb], writes=[rb])
                    S.op("dve", lambda e: e.tensor_tensor(out=rt[:], in0=rt[:], in1=gt[:], op=ALU.mult), reads=[rb, gb], writes=[rb])
                    S.dma("sp", s_mix[HS + n_], rt[:], reads=[rb], writes=[B("mix")])
                S.barrier()

        def phase_mla_proj(l):
            with ExitStack() as ph_:
                T = T1
                ph = {"T": T}
                ph["rs"] = sb(ph_, "m1rs", [128, T], F32)
                ph["rs_b"] = Buf()
                ph["sq"] = [(sb(ph_, "m1sq%d" % i, [128, T], BF16), Buf()) for i in range(3)]
                wqn = sb(ph_, "m1wqn", [128, QC, MH * 128], BF16)
                wqr = sb(ph_, "m1wqr", [128, QC, MH * 64], BF16)
                wkk = sb(ph_, "m1wkk", [128, KVC, MH * 128], BF16)
                wkv = sb(ph_, "m1wkv", [128, KVC, MH * 128], BF16)
                b_w = Buf()
                S.dma("sp", wqn[:], wb_uqn[l], reads=[B("w_uq", l)], writes=[b_w])
                S.dma("sp", wqr[:], wb_uqr[l], reads=[B("w_uq", l)], writes=[b_w])
                S.dma("sp", wkk[:], wb_ukk[l], reads=[B("w_ukv", l)], writes=[b_w])
                S.dma("sp", wkv[:], wb_ukv[l], reads=[B("w_ukv", l)], writes=[b_w])
                cin = sb(ph_, "m1in", [128, QC, T], F32)
                cn = sb(ph_, "m1cn", [128, QC, T], BF16)
                b_in, b_cn = Buf(), Buf()
                cs_t = sb(ph_, "m1cs", [128, 2, T], F32)
                b_cs = Buf()
                stb = [(sb(ph_, "m1sb%d" % i, [128, 512], BF16), Buf()) for i in range(4)]
                zb = [(sb(ph_, "m1zb%d" % i, [128, T], BF16), Buf()) for i in range(2)]
                tmp = [(sb(ph_, "m1tm%d" % i, [128, T], F32), Buf()) for i in range(4)]
                cnt = {"sb": 0, "zb": 0, "tm": 0}

                def rot(lst, k):
                    r = lst[cnt[k] % len(lst)]
                    cnt[k] += 1
                    return r

                for tt in range(NT // T):
                    tok = slice(tt * T, (tt + 1) * T)
                    for ti in range(2):
                        S.dma("sp", cs_t[:, ti, :], tabs[2 + ti][:, tok], reads=[B("tabs")], writes=[b_cs])
                    for (src, nch, gname, nfeat, side) in ((s_cq, QC, "qn", c.QR, "q"), (s_ckv, KVC, "kvn", c.KVR, "kv")):
                        S.dma("sp", cin[:, 0:nch, :], src[:, :, tok].rearrange("kc p t -> p kc t"), reads=[B("scr")], writes=[b_in])
                        acc, fin = ssq_rstd(ph, "rs", nfeat)
                        for kc in range(nch):
                            sq_ap, sq_b = square_to(ph, cin[:, kc, :], [b_in], T)
                            acc(sq_ap, sq_b, kc == 0, kc == nch - 1)
                        fin()
                        for kc in range(nch):
                            S.op("dve", lambda e: e.scalar_tensor_tensor(out=cn[:, kc, :], in0=cin[:, kc, :], scalar=vcol(l, gname, kc), in1=ph["rs"][:, 0:T],
                                                                        op0=ALU.mult, op1=ALU.mult), reads=[b_in, ph["rs_b"], b_vt], writes=[b_cn])

                        def fm(wt, col, nch=nch):
                            pst, psb = PS()
                            for kc in range(nch):
                                S.op("pe", lambda e: e.matmul(pst[:, 0:T], lhsT=wt[:, kc, col:col + 128], rhs=cn[:, kc, :], start=(kc == 0), stop=(kc == nch - 1)),
                                     reads=[b_w, b_cn], writes=[psb], last=(kc == nch - 1))
                            return pst, psb

                        def store_bf(pst, psb, dst):
                            o, ob = rot(stb, "sb")
                            S.op("act", lambda e: e.activation(out=o[:, 0:T], in_=pst[:, 0:T], func=AF.Copy), reads=[psb], writes=[ob])
                            S.dma("sp", dst[:, tok], o[:, 0:T], reads=[ob], writes=[B("scr2")])

                        if side == "q":
                            for h in range(MH):
                                pst, psb = fm(wqn, h * 128)
                                store_bf(pst, psb, s_qn[h])
                            for cc in range(MH // 2):
                                pst, psb = fm(wqr, cc * 128)
                                z, zb_ = rot(zb, "zb")
                                S.op("act", lambda e: e.activation(out=z[:, 0:T], in_=pst[:, 0:T], func=AF.Copy), reads=[psb], writes=[zb_])
                                t1, t1b = rot(tmp, "tm")
                                S.op("dve", lambda e: e.tensor_tensor(out=t1[:, 0:T], in0=pst[:, 0:T], in1=cs_t[:, 0, :], op=ALU.mult), reads=[psb, b_cs], writes=[t1b])
                                ps2, ps2b = PS()
                                S.op("pe", lambda e: e.matmul(ps2[:, 0:T], lhsT=cb[:, c.c_rmla:c.c_rmla + 128], rhs=z[:, 0:T], start=True, stop=True), reads=[zb_, b_cb], writes=[ps2b])
                                t2, t2b = rot(tmp, "tm")
                                S.op("dve", lambda e: e.tensor_tensor(out=t2[:, 0:T], in0=ps2[:, 0:T], in1=cs_t[:, 1, :], op=ALU.mult), reads=[ps2b, b_cs], writes=[t2b])
                                o, ob = rot(stb, "sb")
                                S.op("dve", lambda e: e.tensor_tensor(out=o[:, 0:T], in0=t1[:, 0:T], in1=t2[:, 0:T], op=ALU.add), reads=[t1b, t2b], writes=[ob])
                                S.dma("sp", s_qrp[cc][:, tok], o[:, 0:T], reads=[ob], writes=[B("scr2")])
                        else:
                            for h in range(MH):
                                pst, psb = fm(wkk, h * 128)
                                store_bf(pst, psb, s_kn[h])
                            VG = min(512, MH * 128)
                            for ts in range(T // 128):
                                for v0 in range(0, MH * 128, VG):
                                    pst, psb = PS()
                                    for kc in range(nch):
                                        S.op("pe", lambda e: e.matmul(pst[:, 0:VG], lhsT=cn[:, kc, ts * 128:(ts + 1) * 128], rhs=wkv[:, kc, v0:v0 + VG], start=(kc == 0), stop=(kc == nch - 1)),
                                             reads=[b_w, b_cn], writes=[psb], last=(kc == nch - 1))
                                    o, ob = rot(stb, "sb")
                                    S.op("act", lambda e: e.activation(out=o[:, 0:VG], in_=pst[:, 0:VG], func=AF.Copy), reads=[psb], writes=[ob])
                                    S.dma("sp", s_vc[tt * T + ts * 128:tt * T + (ts + 1) * 128, v0:v0 + VG], o[:, 0:VG], reads=[ob], writes=[B("scr2")])
                S.barrier()

        def phase_mla_attn(l):
            with ExitStack() as ph_:
                T = T1
                krt = sb(ph_, "makr", [128, NT], BF16)
                b_kr = Buf()
                S.dma("sp", krt[:], s_kr[:, :], reads=[B("scr")], writes=[b_kr])
                qn_t = [(sb(ph_, "maqn%d" % i, [128, NT], BF16), Buf()) for i in range(2)]
                qr_t = [(sb(ph_, "maqr%d" % i, [128, NT], BF16), Buf()) for i in range(2)]
                kn_t = [(sb(ph_, "makn%d" % i, [128, NT], BF16), Buf()) for i in range(2)]
                v_t = [(sb(ph_, "mav%d" % i, [128, NB, 128], BF16), Buf()) for i in range(2)]
                ets = [(sb(ph_, "mae%d" % i, [128, 512], BF16), Buf()) for i in range(4)]
                rcs = [(sb(ph_, "mar%d" % i, [128, 512], F32), Buf()) for i in range(2)]
                ots = [(sb(ph_, "mao%d" % i, [128, 512], F32), Buf()) for i in range(2)]
                scale = 192.0 ** -0.5
                ne = [0, 0, 0, 0]
                def load_head(h_):
                    S.dma("sp", qn_t[h_ % 2][0][:], s_qn[h_], reads=[B("scr2")], writes=[qn_t[h_ % 2][1]])
                    S.dma("sp", kn_t[h_ % 2][0][:], s_kn[h_], reads=[B("scr2")], writes=[kn_t[h_ % 2][1]])
                    S.dma("sp", v_t[h_ % 2][0][:], s_vc[:, h_ * 128:(h_ + 1) * 128].rearrange("(nb p) d -> p nb d", p=128), reads=[B("scr2")], writes=[v_t[h_ % 2][1]])
                    if h_ % 2 == 0:
                        S.dma("sp", qr_t[(h_ // 2) % 2][0][:], s_qrp[h_ // 2], reads=[B("scr2")], writes=[qr_t[(h_ // 2) % 2][1]])
                load_head(0)
                for h in range(MH):
                    qn, qnb = qn_t[h % 2]
                    kn, knb = kn_t[h % 2]
                    vv, vb = v_t[h % 2]
                    qr, qrb = qr_t[(h // 2) % 2]
                    if h + 1 < MH:
                        load_head(h + 1)
                    hp = (h % 2) * 64
                    for qt in range(NT // T):
                        q0 = qt * T
                        nkb = (q0 + T) // 128
                        pso, psob = psum[4 + 2 * (ne[3] % 2)]
                        psd, psdb = psum[5 + 2 * (ne[3] % 2)]
                        ne[3] += 1
                        def scores(kb):
                            lo = max(0, kb * 128 - q0)
                            diag = kb * 128 >= q0
                            pst, psb = PS(0, 4)
                            ks = slice(kb * 128, (kb + 1) * 128)
                            S.op("pe", lambda e: e.matmul(pst[:, lo:T], lhsT=kn[:, ks], rhs=qn[:, q0 + lo:q0 + T], start=True, stop=False),
                                 reads=[knb, qnb], writes=[psb], last=False)
                            S.op("pe", lambda e: e.matmul(pst[:, lo:T], lhsT=krt[hp:hp + 64, ks], rhs=qr[hp:hp + 64, q0 + lo:q0 + T], start=False, stop=(not diag)),
                                 reads=[b_kr, qrb], writes=[psb], last=(not diag))
                            if diag:
                                S.op("pe", lambda e: e.matmul(pst[:, lo:lo + 128], lhsT=id_b, rhs=cb[:, c.c_negtri:c.c_negtri + 128], start=False, stop=True),
                                     reads=[b_cb], writes=[psb], last=True)
                            return pst, psb, lo

                        def rest(kb, pst, psb, lo):
                            et, eb = ets[ne[0] % 4]; ne[0] += 1
                            S.op("act", lambda e: e.activation(out=et[:, lo:T], in_=pst[:, lo:T], func=AF.Exp, scale=scale), reads=[psb], writes=[eb])
                            S.op("pe", lambda e: e.matmul(pso[:, lo:T], lhsT=vv[:, kb, :], rhs=et[:, lo:T], start=(kb == 0), stop=(kb == nkb - 1)),
                                 reads=[vb, eb], writes=[psob], last=False)
                            S.op("pe", lambda e: e.matmul(psd[:, lo:T], lhsT=ones_b, rhs=et[:, lo:T], start=(kb == 0), stop=(kb == nkb - 1)),
                                 reads=[b_cb, eb], writes=[psdb], last=True)

                        pend = [scores(kb) for kb in range(min(2, nkb))]
                        for kb in range(nkb):
                            if kb + 2 < nkb:
                                pend.append(scores(kb + 2))
                            rest(kb, *pend.pop(0))
                        rc, rb = rcs[ne[1] % 2]; ne[1] += 1
                        ot, ob = ots[ne[2] % 2]; ne[2] += 1
                        S.op("dve", lambda e: e.reciprocal(out=rc[:, 0:T], in_=psd[:, 0:T]), reads=[psdb], writes=[rb])
                        S.op("dve", lambda e: e.tensor_tensor(out=ot[:, 0:T], in0=pso[:, 0:T], in1=rc[:, 0:T], op=ALU.mult), reads=[psob, rb], writes=[ob])
                        S.dma("sp", s_mix[HS + RC + h][:, q0:q0 + T], ot[:, 0:T], reads=[ob], writes=[B("mix")])
                S.barrier()

        def phase_rows(l, xcur, xcur_key, xnext, xnext_key):
            with ExitStack() as ph_:
                T = T2
                ph = {"T": T}
                ph["rs"] = sb(ph_, "r5rs", [128, T], F32)
                ph["rs_b"] = Buf()
                ph["rs2"] = sb(ph_, "r5rs2", [128, T], F32)
                ph["rs2_b"] = Buf()
                ph["sq"] = [(sb(ph_, "r5sq%d" % i, [128, T], BF16), Buf()) for i in range(3)]
                yt = sb(ph_, "r5y", [128, DC, T], F32)
                hT = sb(ph_, "r5h", [128, DC, T], BF16)
                aT = sb(ph_, "r5a", [128, FH, T], BF16)
                pbb = sb(ph_, "r5pb", [128, PC, T], BF16)
                b_h, b_a, b_pb = Buf(), Buf(), Buf()
                by = [Buf() for _ in range(DC)]
                dq_ = []
                WSZ = max(FH * 128, DC * 128, PC * 128)
                NWB = 3
                wts = [(sb(ph_, "r5w%d" % i, [128, WSZ], BF16), Buf()) for i in range(NWB)]
                tmp = [(sb(ph_, "r5tm%d" % i, [128, T], F32), Buf()) for i in range(3)]
                xin = [(sb(ph_, "r5xi%d" % i, [128, T], F32), Buf()) for i in range(4)]
                cnt = {"w": 0, "tm": 0, "xi": 0}

                def rot(lst, k):
                    r = lst[cnt[k] % len(lst)]
                    cnt[k] += 1
                    return r

                def ldw(src, key, ncol):
                    def f():
                        wt, wb_ = rot(wts, "w")
                        S.dma("sp", wt[:, 0:ncol], src.rearrange("p k n -> p (k n)"), reads=[B(*key)], writes=[wb_])
                        return wt, wb_
                    return f

                for tt in range(NT // T):
                    tok = slice(tt * T, (tt + 1) * T)

                    def xchunk(ap, oc):
                        return ap[oc * 128:(oc + 1) * 128, tok]

                    def resid(gname, xsrc, ksrc, xdst, kdst, next_sq=False, to_bf=False):
                        if next_sq:
                            acc, fin = ssq_rstd(ph, "rs2", D)
                        xq = []

                        def xload(o_):
                            xi_, xib_ = rot(xin, "xi")
                            S.dma("sp", xi_[:, 0:T], xchunk(xsrc, o_), reads=[B(ksrc)], writes=[xib_])
                            xq.append((xi_, xib_))
                        for o_ in range(min(2, DC)):
                            xload(o_)
                        for oc in range(DC):
                            t, tb = rot(tmp, "tm")
                            S.op("dve", lambda e: e.scalar_tensor_tensor(out=t[:, 0:T], in0=yt[:, oc, :], scalar=vcol(l, gname, oc), in1=ph["rs"][:, 0:T], op0=ALU.mult, op1=ALU.mult),
                                 reads=[by[oc], ph["rs_b"], b_vt], writes=[tb])
                            xi, xib = xq.pop(0)
                            if oc + 2 < DC:
                                xload(oc + 2)
                            S.op("dve", lambda e: e.tensor_tensor(out=yt[:, oc, :], in0=xi[:, 0:T], in1=t[:, 0:T], op=ALU.add), reads=[tb, xib, by[oc]], writes=[by[oc]])
                            S.dma("sp", xchunk(xdst, oc), yt[:, oc, :], reads=[by[oc]], writes=[B(kdst)])
                            if next_sq:
                                sq_ap, sq_b = square_to(ph, yt[:, oc, :], [by[oc]], T)
                                acc(sq_ap, sq_b, oc == 0, oc == DC - 1)
                            if to_bf:
                                S.op("act", lambda e: e.activation(out=hT[:, oc, :], in_=yt[:, oc, :], func=AF.Copy), reads=[by[oc]], writes=[b_h])
                        if next_sq:
                            fin()

                    CH = 8 if DC >= 8 else DC

                    def load_mix(k0, tk):
                        S.dma("sp", yt[:, k0:k0 + CH, :], s_mix[k0:k0 + CH, :, tk].rearrange("kc p t -> p kc t"), reads=[B("mix")], writes=by[k0:k0 + CH])

                    def load_p(tk):
                        for kc in range(PC):
                            xi_, xib_ = rot(xin, "xi")
                            S.dma("sp", xi_[:, 0:T], pT[l][kc * 128:(kc + 1) * 128, tk], writes=[xib_])
                            S.op("dve", lambda e: e.tensor_copy(out=pbb[:, kc, :], in_=xi_[:, 0:T]), reads=[xib_], writes=[b_pb])
                    if tt == 0:
                        for k0 in range(0, DC, CH):
                            load_mix(k0, tok)
                        load_p(tok)
                    has_next = tt + 1 < NT // T
                    tokn = slice((tt + 1) * T, (tt + 2) * T)
                    for gi, (c0, c1_) in enumerate(((0, HS), (HS, HS + RC), (HS + RC, DC))):
                        acc, fin = ssq_rstd(ph, "rs", (c1_ - c0) * 128)
                        for kc in range(c0, c1_):
                            sq_ap, sq_b = square_to(ph, yt[:, kc, :], [by[kc]], T)
                            acc(sq_ap, sq_b, kc == c0, kc == c1_ - 1)
                        fin()
                        for kc in range(c0, c1_):
                            S.op("dve", lambda e: e.scalar_tensor_tensor(out=hT[:, kc, :], in0=yt[:, kc, :], scalar=vcol(l, "gn", kc), in1=ph["rs"][:, 0:T], op0=ALU.mult, op1=ALU.mult),
                                 reads=[by[kc], ph["rs_b"], b_vt], writes=[b_h])

                    def dense_to_y(wb_t, key, src_bf, src_buf, nkc):
                        acc, fin = ssq_rstd(ph, "rs", D)
                        items = []
                        for oc in range(DC):
                            def cp(hd, oc=oc):
                                wt, wb_ = hd
                                pst, psb = PS()
                                for kc in range(nkc):
                                    S.op("pe", lambda e: e.matmul(pst[:, 0:T], lhsT=wt[:, kc * 128:(kc + 1) * 128], rhs=src_bf[:, kc, :], start=(kc == 0), stop=(kc == nkc - 1)),
                                         reads=[wb_, src_buf], writes=[psb], last=(kc == nkc - 1))
                                while dq_:
                                    dq_.pop(0)()
                                S.op("dve", lambda e: e.tensor_copy(out=yt[:, oc, :], in_=pst[:, 0:T]), reads=[psb], writes=[by[oc]])
                                sq_ap, sq_b = square_to(ph, pst[:, 0:T], [psb], T)
                                dq_.append(lambda: acc(sq_ap, sq_b, oc == 0, oc == DC - 1))
                            items.append((ldw(wb_t[l, oc], key + (l, oc), nkc * 128), cp))
                        pipeline(items, 2)
                        while dq_:
                            dq_.pop(0)()
                        fin()

                    dense_to_y(wb_out, ("w_out",), hT, b_h, DC)
                    resid("post_mix", xcur, xcur_key, xs1, "xs1", next_sq=True)
                    for kc in range(DC):
                        S.op("dve", lambda e: e.scalar_tensor_tensor(out=hT[:, kc, :], in0=yt[:, kc, :], scalar=vcol(l, "pre_ffn", kc), in1=ph["rs2"][:, 0:T], op0=ALU.mult, op1=ALU.mult),
                             reads=[by[kc], ph["rs2_b"], b_vt], writes=[b_h])
                    for hh in range(2):
                        items = []
                        for k in range(FH):
                            f = hh * FH + k
                            for (wsrc, key) in ((wb_gate, "w_gate"), (wb_up, "w_up")):
                                def cp(hd, k=k, key=key):
                                    wt, wb_ = hd
                                    pst, psb = PS()
                                    for kc in range(DC):
                                        S.op("pe", lambda e: e.matmul(pst[:, 0:T], lhsT=wt[:, kc * 128:(kc + 1) * 128], rhs=hT[:, kc, :], start=(kc == 0), stop=(kc == DC - 1)),
                                             reads=[wb_, b_h], writes=[psb], last=(kc == DC - 1))
                                    if key == "w_gate":
                                        t, tb = rot(tmp, "tm")
                                        S.op("act", lambda e: e.activation(out=t[:, 0:T], in_=pst[:, 0:T], func=AF.Silu), reads=[psb], writes=[tb])
                                        ph["gate"] = (t, tb)
                                    else:
                                        t, tb = ph["gate"]
                                        S.op("dve", lambda e: e.tensor_tensor(out=aT[:, k, :], in0=pst[:, 0:T], in1=t[:, 0:T], op=ALU.mult), reads=[psb, tb], writes=[b_a])
                                items.append((ldw(wsrc[l, f], (key, l, f), DC * 128), cp))
                        pipeline(items, 2)
                        if hh == 1:
                            acc, fin = ssq_rstd(ph, "rs", D)
                        items = []
                        for oc in range(DC):
                            def cp(hd, oc=oc, hh=hh):
                                wt, wb_ = hd
                                pst, psb = PS()
                                for k in range(FH):
                                    S.op("pe", lambda e: e.matmul(pst[:, 0:T], lhsT=wt[:, k * 128:(k + 1) * 128], rhs=aT[:, k, :], start=(k == 0), stop=(k == FH - 1)),
                                         reads=[wb_, b_a], writes=[psb], last=(k == FH - 1))
                                while dq_:
                                    dq_.pop(0)()
                                if hh == 0:
                                    S.op("dve", lambda e: e.tensor_copy(out=yt[:, oc, :], in_=pst[:, 0:T]), reads=[psb], writes=[by[oc]])
                                else:
                                    S.op("dve", lambda e: e.tensor_tensor(out=yt[:, oc, :], in0=pst[:, 0:T], in1=yt[:, oc, :], op=ALU.add), reads=[psb, by[oc]], writes=[by[oc]])
                                    sq_ap, sq_b = square_to(ph, yt[:, oc, :], [by[oc]], T)
                                    dq_.append(lambda: acc(sq_ap, sq_b, oc == 0, oc == DC - 1))
                            items.append((ldw(wb_down[l, oc, hh], ("w_down", l, oc, hh), FH * 128), cp))
                        pipeline(items, 2)
                    while dq_:
                        dq_.pop(0)()
                    fin()
                    resid("post_ffn", xs1, "xs1", xs2, "xs2", to_bf=True)
                    dense_to_y(wb_ple, ("w_ple",), pbb, b_pb, PC)
                    for oc in range(DC):
                        S.op("dve", lambda e: e.scalar_tensor_tensor(out=yt[:, oc, :], in0=yt[:, oc, :], scalar=vcol(l, "ple_n", oc), in1=ph["rs"][:, 0:T], op0=ALU.mult, op1=ALU.mult),
                             reads=[by[oc], ph["rs_b"], b_vt], writes=[by[oc]])
                    items = []
                    for oc in range(DC):
                        def cp(hd, oc=oc):
                            wt, wb_ = hd
                            pst, psb = PS()
                            for kc in range(DC):
                                S.op("pe", lambda e: e.matmul(pst[:, 0:T], lhsT=wt[:, kc * 128:(kc + 1) * 128], rhs=hT[:, kc, :], start=(kc == 0), stop=(kc == DC - 1)),
                                     reads=[wb_, b_h], writes=[psb], last=(kc == DC - 1))
                            t, tb = rot(tmp, "tm")
                            S.op("act", lambda e: e.activation(out=t[:, 0:T], in_=pst[:, 0:T], func=AF.Sigmoid, bias=vcol(l, "bpg", oc)), reads=[psb, b_vt], writes=[tb])
                            S.op("dve", lambda e: e.tensor_tensor(out=t[:, 0:T], in0=t[:, 0:T], in1=yt[:, oc, :], op=ALU.mult), reads=[tb, by[oc]], writes=[tb])
                            xi, xib = rot(xin, "xi")
                            S.dma("sp", xi[:, 0:T], xchunk(xs2, oc), reads=[B("xs2")], writes=[xib])
                            S.op("dve", lambda e: e.tensor_tensor(out=xi[:, 0:T], in0=xi[:, 0:T], in1=t[:, 0:T], op=ALU.add), reads=[tb, xib], writes=[xib])
                            S.dma("sp", xchunk(xnext, oc), xi[:, 0:T], reads=[xib], writes=[B(xnext_key)])
                            if has_next and (oc + 1) % CH == 0:
                                load_mix(oc + 1 - CH, tokn)
                            if has_next and oc == DC - 1:
                                load_p(tokn)
                        items.append((ldw(wb_pg[l, oc], ("w_pg", l, oc), DC * 128), cp))
                    pipeline(items, 2)
                S.barrier()

        def on(p):
            return phases is None or p in phases
        cast_layer_mixer(0)
        cast_layer_rows(0)
        if on("tabs"):
            setup_tables()
        for l in range(L):
            xcur, xck = (xT, "x_in") if l == 0 else (xres, "xres")
            xnext, xnk = (yT, "y_out") if l == L - 1 else (xres, "xres")
            if on("p1"):
                phase_p1(l, xcur, xck)
            if l + 1 < L:
                cast_layer_mixer(l + 1)
            if on("swa"):
                phase_swa(l)
            if on("rg"):
                phase_rg(l)
            if on("mlap"):
                phase_mla_proj(l)
            if on("mlaa"):
                phase_mla_attn(l)
            if on("rows"):
                phase_rows(l, xcur, xck, xnext, xnk)
            if l + 1 < L:
                cast_layer_rows(l + 1)
            if phases is not None and "l0" in phases:
                break
        S.barrier(final=True)
    return nc


def _consts(cfg):
    c = cfg
    cf = np.zeros((128, c.CW), np.float32)
    cf[:, c.c_ones:c.c_ones + 128] = 1.0
    cf[:, c.c_id:c.c_id + 128] = np.eye(128, dtype=np.float32)
    R = np.zeros((128, 128), np.float32)
    for m in range(128):
        if m < 64: R[m, m + 64] = -1.0
        else: R[m, m - 64] = 1.0
    cf[:, c.c_rswa:c.c_rswa + 128] = R.T
    R = np.zeros((128, 128), np.float32)
    for m in range(128):
        if (m % 64) < 32: R[m, m + 32] = -1.0
        else: R[m, m - 32] = 1.0
    cf[:, c.c_rmla:c.c_rmla + 128] = R.T
    k = np.arange(128)[:, None]
    q = np.arange(128)[None, :]
    prev = (k > q).astype(np.float32)
    cur = (k <= q).astype(np.float32)
    m = np.concatenate([prev, cur, prev, cur], axis=1)
    cf[:, c.c_mask:c.c_mask + 512] = m
    m0 = m.copy()
    m0[:, 0:128] = 0.0
    cf[:, c.c_mask0:c.c_mask0 + 512] = m0
    cf[:, c.c_negtri:c.c_negtri + 128] = np.where(k <= q, 0.0, -30000.0).astype(np.float32)
    p = np.arange(128)
    cf[:, c.c_invs] = (THETA ** (-(2.0 * (p % 64)) / 128.0)).astype(np.float32)
    cf[:, c.c_invm] = (THETA ** (-(2.0 * (p % 32)) / 64.0)).astype(np.float32)
    return cf


def _pack_vecs(cfg, inp, l):
    c = cfg
    v = np.zeros((128, c.NV), np.float32)

    def put(name, arr):
        arr = np.asarray(arr, np.float32).reshape(-1, 128).T
        v[:, c.voff[name]:c.voff[name] + arr.shape[1]] = arr
    put("pre_mix", inp["pre_mix_norm"][l])
    for j in range(4):
        put("cw%d" % j, inp["rg_conv_w"][l, j])
    put("cb", inp["rg_conv_b"][l])
    put("ba", inp["rg_gate_a_b"][l])
    put("bx", inp["rg_gate_x_b"][l])
    put("lam", inp["rg_lambda"][l])
    put("qn", inp["mla_q_norm"][l])
    put("kvn", inp["mla_kv_norm"][l])
    put("gn", inp["group_norm"][l])
    put("post_mix", inp["post_mix_norm"][l])
    put("pre_ffn", inp["pre_ffn_norm"][l])
    put("post_ffn", inp["post_ffn_norm"][l])
    put("ple_n", inp["ple_norm"][l])
    put("bpg", inp["b_ple_gate"][l])
    return v


def make_in_maps(cfg, inp, ncores):
    c = cfg
    L = c.DEPTH
    f = lambda a: np.ascontiguousarray(np.asarray(a, np.float32))
    shared = {
        "vecs": np.stack([_pack_vecs(c, inp, l) for l in range(L)]),
        "sinkb": np.ascontiguousarray(np.broadcast_to(np.asarray(inp["swa_sinks"], np.float32)[:, None, :], (L, 128, c.HS))),
        "consts": _consts(c),
        "w_in": f(inp["w_in"]), "w_ga": f(inp["rg_gate_a_w"]), "w_gx": f(inp["rg_gate_x_w"]),
        "w_uq": f(inp["mla_w_uq"]), "w_ukv": f(inp["mla_w_ukv"]), "w_out": f(inp["w_out"]),
        "w_gate": f(inp["w_gate"]), "w_up": f(inp["w_up"]), "w_down": f(inp["w_down"]),
        "w_ple": f(inp["w_ple"]), "w_pg": f(inp["w_ple_gate"]),
    }
    x = np.asarray(inp["x"], np.float32)
    p = np.asarray(inp["p"], np.float32)
    pos = np.asarray(inp["positions"]).astype(np.int32)
    maps = []
    for b in range(ncores):
        m = dict(shared)
        m["xT"] = np.ascontiguousarray(x[b].T)
        m["pT"] = np.ascontiguousarray(np.transpose(p[:, b], (0, 2, 1)))
        m["posb"] = np.ascontiguousarray(np.broadcast_to(pos[b][None, :], (128, c.NT)))
        maps.append(m)
    return maps


def kernel(**inputs):
    cfg = Cfg()
    nb = np.asarray(inputs["x"]).shape[0]
    nc = build(cfg)
    maps = make_in_maps(cfg, inputs, nb)
    res = run_bass_kernel_spmd(nc, maps, core_ids=list(range(nb)))
    out = np.stack([np.ascontiguousarray(r["yT"].T) for r in res.results]).astype(np.float32)
    return out
```

```python
import math
from contextlib import ExitStack
import numpy as np
import concourse.bass as bass
import concourse.mybir as mybir
from concourse.bass_utils import run_bass_kernel_spmd

dt = mybir.dt
F32, BF16, I32 = dt.float32, dt.bfloat16, dt.int32
AF = mybir.ActivationFunctionType
ALU = mybir.AluOpType
EPS = 1e-6
THETA = 10000.0


class Cfg:
    def __init__(s, D=4096, NT=2048, DFF=11008, HS=12, KV=4, RGW=1024, MH=12, QR=1024, KVR=512, PLE=256, DEPTH=2):
        s.D, s.NT, s.DFF, s.HS, s.KV, s.RGW, s.MH, s.QR, s.KVR, s.PLE, s.DEPTH = D, NT, DFF, HS, KV, RGW, MH, QR, KVR, PLE, DEPTH
        s.DC = D // 128
        s.FC = DFF // 128
        s.RC = RGW // 128
        s.QC = QR // 128
        s.KVC = KVR // 128
        s.PC = PLE // 128
        s.NB = NT // 128
        s.G = HS // KV
        s.INW = HS * 128 + 2 * KV * 128 + 2 * RGW + QR + KVR + 64
        assert HS * 128 + RGW + MH * 128 == D
        o = 0
        s.o_qa = o; o += HS * 128
        s.o_ka = o; o += KV * 128
        s.o_va = o; o += KV * 128
        s.o_xr = o; o += RGW
        s.o_gr = o; o += RGW
        s.o_cq = o; o += QR
        s.o_ckv = o; o += KVR
        s.o_kr = o; o += 64
        names = [("pre_mix", s.DC), ("cw0", s.RC), ("cw1", s.RC), ("cw2", s.RC), ("cw3", s.RC), ("cb", s.RC),
                 ("ba", s.RC), ("bx", s.RC), ("lam", s.RC), ("qn", s.QC), ("kvn", s.KVC), ("gn", s.DC),
                 ("post_mix", s.DC), ("pre_ffn", s.DC), ("post_ffn", s.DC), ("ple_n", s.DC), ("bpg", s.DC)]
        s.voff = {}
        o = 0
        for n, c in names:
            s.voff[n] = o
            o += c
        s.NV = o
        s.c_ones, s.c_id, s.c_rswa, s.c_rmla = 0, 128, 256, 384
        s.c_mask, s.c_mask0, s.c_negtri = 512, 1024, 1536
        s.c_invs, s.c_invm = 1664, 1665
        s.CW = 1666


class Buf:
    __slots__ = ("w", "r", "excl")

    def __init__(s, excl=False):
        s.w = {}
        s.r = {}
        s.excl = excl


class Sched:
    def __init__(s, nc, es, ndma=8):
        s.nc = nc
        s.eng = {"pe": nc.tensor, "act": nc.scalar, "dve": nc.vector, "pool": nc.gpsimd, "sp": nc.sync}
        s.sems = []
        s.cs = {}
        s.cnt = {}
        for e in ("pe", "act", "dve", "pool"):
            s.cs[e] = len(s.sems)
            s.sems.append(es.enter_context(nc.semaphore("c_" + e)))
            s.cnt[e] = 0
        s.dq = {}
        s.dqn = {}
        s.first_dma = len(s.sems)
        for q in ("sp", "pool"):
            s.dq[q] = []
            for i in range(ndma if q == "sp" else 4):
                s.dq[q].append([len(s.sems), 0])
                s.sems.append(es.enter_context(nc.semaphore("d_%s%d" % (q, i))))
            s.dqn[q] = 0
        s.waited = {e: {} for e in s.eng}
        s.pe_open = False
        s.nwait = 0

    def _wait(s, e, si, val):
        if s.waited[e].get(si, 0) >= val:
            return
        s.eng[e].wait_ge(s.sems[si], val)
        s.waited[e][si] = val
        s.nwait += 1

    def _deps(s, e, reads, writes, is_dma=False):
        need = {}
        for b in reads:
            for si, v in b.w.items():
                if need.get(si, 0) < v:
                    need[si] = v
            if b.excl:
                for si, v in b.r.items():
                    if need.get(si, 0) < v:
                        need[si] = v
        for b in writes:
            for si, v in b.w.items():
                if is_dma and si >= s.first_dma:
                    continue
                if need.get(si, 0) < v:
                    need[si] = v
            for si, v in b.r.items():
                if need.get(si, 0) < v:
                    need[si] = v
        for si, v in need.items():
            if e == "pe" and si == s.cs["pe"]:
                continue
            s._wait(e, si, v)

    def _mark(s, ev, reads, writes, is_dma=False):
        for b in writes:
            if is_dma:
                b.w[ev[0]] = ev[1]
            else:
                b.w = {ev[0]: ev[1]}
            b.r = {}
        for b in reads:
            if b.r.get(ev[0], 0) < ev[1]:
                b.r[ev[0]] = ev[1]

    def op(s, e, fn, reads=(), writes=(), last=True):
        s._deps(e, reads, writes)
        ins = fn(s.eng[e])
        if e == "pe" and not last:
            ev = (s.cs[e], s.cnt[e] + 1)
        else:
            s.cnt[e] += 1
            ins.then_inc(s.sems[s.cs[e]], 1)
            ev = (s.cs[e], s.cnt[e])
        s._mark(ev, reads, writes)
        return ins

    def dma(s, q, out, in_, reads=(), writes=(), **kw):
        slots = s.dq[q]
        slot = slots[s.dqn[q] % len(slots)]
        s.dqn[q] += 1
        if slot[1] > 0:
            s._wait(q, slot[0], slot[1])
        s._deps(q, reads, writes, is_dma=True)
        ins = s.eng[q].dma_start(out=out, in_=in_, **kw)
        slot[1] += 16
        ins.then_inc(s.sems[slot[0]], 16)
        s._mark((slot[0], slot[1]), reads, writes, is_dma=True)

    def barrier(s, final=False):
        for e in s.eng:
            for ce in ("pe", "act", "dve", "pool"):
                if s.cnt[ce] > 0:
                    s._wait(e, s.cs[ce], s.cnt[ce])
            for q in s.dq:
                if q == "pool" and not final:
                    continue
                for si, v in s.dq[q]:
                    if v > 0:
                        s._wait(e, si, v)


def build(cfg, dbg=False, phases=None):
    c = cfg
    nc = bass.Bass("TRN2", target_bir_lowering=False)
    D, NT, DC, FC, RC, QC, KVC, PC, NB, HS, KV, MH, L = c.D, c.NT, c.DC, c.FC, c.RC, c.QC, c.KVC, c.PC, c.NB, c.HS, c.KV, c.MH, c.DEPTH
    T1 = min(512, NT)
    T2 = min(512, NT)
    GW = 256

    def din(name, shape, d=F32):
        return nc.dram_tensor(name, list(shape), d, kind="ExternalInput").ap()

    def dscr(name, shape, d=F32):
        kind = "ExternalOutput" if (dbg and name in dbg) else "Internal"
        return nc.dram_tensor(name, list(shape), d, kind=kind).ap()

    xT = din("xT", [D, NT])
    pT = din("pT", [L, c.PLE, NT])
    posb = din("posb", [128, NT], I32)
    vecs = din("vecs", [L, 128, c.NV])
    sinkb = din("sinkb", [L, 128, HS])
    consts = din("consts", [128, c.CW])
    w_in = din("w_in", [L, D, c.INW])
    w_ga = din("w_ga", [L, RC, 128, 128])
    w_gx = din("w_gx", [L, RC, 128, 128])
    w_uq = din("w_uq", [L, c.QR, MH * 192])
    w_ukv = din("w_ukv", [L, c.KVR, MH * 256])
    w_out = din("w_out", [L, D, D])
    w_gate = din("w_gate", [L, D, c.DFF])
    w_up = din("w_up", [L, D, c.DFF])
    w_down = din("w_down", [L, c.DFF, D])
    w_ple = din("w_ple", [L, c.PLE, D])
    w_pg = din("w_pg", [L, D, D])
    yT = nc.dram_tensor("yT", [D, NT], F32, kind="ExternalOutput").ap()

    NG_IN = (c.INW - 64) // GW
    assert (c.INW - 64) % GW == 0
    wb_in = dscr("wb_in", [L, NG_IN, 128, DC, GW], BF16)
    wb_kr = dscr("wb_kr", [L, 128, DC, 128], BF16)
    wb_ga = dscr("wb_ga", [L, 128, RC, 128], BF16)
    wb_gx = dscr("wb_gx", [L, 128, RC, 128], BF16)
    wb_uqn = dscr("wb_uqn", [L, 128, QC, MH * 128], BF16)
    wb_uqr = dscr("wb_uqr", [L, 128, QC, MH * 64], BF16)
    wb_ukk = dscr("wb_ukk", [L, 128, KVC, MH * 128], BF16)
    wb_ukv = dscr("wb_ukv", [L, 128, KVC, MH * 128], BF16)
    wb_out = dscr("wb_out", [L, DC, 128, DC, 128], BF16)
    wb_gate = dscr("wb_gate", [L, FC, 128, DC, 128], BF16)
    wb_up = dscr("wb_up", [L, FC, 128, DC, 128], BF16)
    FH = FC // 2
    assert FC % 2 == 0
    wb_down = dscr("wb_down", [L, DC, 2, 128, FH, 128], BF16)
    wb_ple = dscr("wb_ple", [L, DC, 128, PC, 128], BF16)
    wb_pg = dscr("wb_pg", [L, DC, 128, DC, 128], BF16)
    xres = dscr("xres", [D, NT])
    xs1 = dscr("xs1", [D, NT])
    xs2 = dscr("xs2", [D, NT])
    tabs = dscr("tabs", [4, 128, NT])
    s_qa = dscr("s_qa", [HS, 128, NT], BF16)
    s_ka = dscr("s_ka", [KV, 128, NT], BF16)
    s_va = dscr("s_va", [NT, KV * 128], BF16)
    s_xr = dscr("s_xr", [RC, 128, NT])
    s_gr = dscr("s_gr", [RC, 128, NT])
    s_cq = dscr("s_cq", [QC, 128, NT])
    s_ckv = dscr("s_ckv", [KVC, 128, NT])
    s_kr = dscr("s_kr", [128, NT], BF16)
    s_mix = dscr("s_mix", [DC, 128, NT])
    s_qn = dscr("s_qn", [MH, 128, NT], BF16)
    s_qrp = dscr("s_qrp", [MH // 2, 128, NT], BF16)
    s_kn = dscr("s_kn", [MH, 128, NT], BF16)
    s_vc = dscr("s_vc", [NT, MH * 128], BF16)

    es = ExitStack()
    with es:
        S = Sched(nc, es)
        bufs = {}

        def B(*key):
            if key not in bufs:
                bufs[key] = Buf()
            return bufs[key]

        uniq = [0]

        def sb(es_, name, shape, d):
            uniq[0] += 1
            t = es_.enter_context(nc.sbuf_tensor("%s_%d" % (name, uniq[0]), list(shape), d))
            return t

        psum = []
        for i in range(8):
            t = es.enter_context(nc.psum_tensor("ps%d" % i, [128, 512], F32))
            psum.append((t, Buf(excl=True)))
        psn = [0]

        def PS(lo=0, n=6):
            r = psum[lo + psn[0] % n]
            psn[0] += 1
            return r

        cf = sb(es, "cf", [128, c.CW], F32)
        cb = sb(es, "cb", [128, c.CW], BF16)
        vt = sb(es, "vt", [128, L, c.NV], F32)
        c1t = sb(es, "c1t", [128, L, RC], F32)
        skt = sb(es, "skt", [128, L, HS], F32)
        b_cf, b_cb, b_vt, b_c1, b_sk = Buf(), Buf(), Buf(), Buf(), Buf()
        S.dma("sp", cf[:], consts[:, :], writes=[b_cf])
        S.op("dve", lambda e: e.tensor_copy(out=cb[:], in_=cf[:]), reads=[b_cf], writes=[b_cb])
        for l in range(L):
            S.dma("sp", vt[:, l, :], vecs[l], writes=[b_vt])
            S.dma("sp", skt[:, l, :], sinkb[l], writes=[b_sk])
        S.op("act", lambda e: e.activation(out=skt[:], in_=skt[:], func=AF.Exp), reads=[b_sk], writes=[b_sk])
        for l in range(L):
            lo = c.voff["lam"]
            S.op("act", lambda e: e.activation(out=c1t[:, l, :], in_=vt[:, l, lo:lo + RC], func=AF.Exp, scale=-1.0), reads=[b_vt], writes=[b_c1])
        S.op("act", lambda e: e.activation(out=c1t[:], in_=c1t[:], func=AF.Ln, bias=1.0), reads=[b_c1], writes=[b_c1])
        S.op("dve", lambda e: e.tensor_scalar(out=c1t[:], in0=c1t[:], scalar1=-8.0, scalar2=None, op0=ALU.mult), reads=[b_c1], writes=[b_c1])
        ones_b = cb[:, c.c_ones:c.c_ones + 128]
        id_b = cb[:, c.c_id:c.c_id + 128]

        def vcol(l, name, j):
            o = c.voff[name] + j
            return vt[:, l, o:o + 1]

        def cast(dst, src, wkey):
            S.dma("pool", dst, src, writes=[B(*wkey)], max_dma_last_dim=4096)

        def cast_layer_mixer(l):
            for g in range(NG_IN):
                cast(wb_in[l, g], w_in[l][:, g * GW:(g + 1) * GW].rearrange("(kc p) n -> p kc n", p=128), ("w_in", l, g))
            for h in range(2):
                cast(wb_kr[l][:, :, h * 64:(h + 1) * 64], w_in[l][:, c.o_kr:c.o_kr + 64].rearrange("(kc p) n -> p kc n", p=128), ("w_kr", l))
            cast(wb_ga[l], w_ga[l].rearrange("n c d -> c n d"), ("w_g", l))
            cast(wb_gx[l], w_gx[l].rearrange("n c d -> c n d"), ("w_g", l))
            uq = w_uq[l].rearrange("(kc p) (h c) -> p kc h c", p=128, c=192)
            for kc in range(QC):
                cast(wb_uqn[l][:, kc, :].rearrange("p (h c) -> p h c", c=128), uq[:, kc, :, 0:128], ("w_uq", l))
                cast(wb_uqr[l][:, kc, :].rearrange("p (h c) -> p h c", c=64), uq[:, kc, :, 128:192], ("w_uq", l))
            ukv = w_ukv[l].rearrange("(kc p) (h c) -> p kc h c", p=128, c=256)
            for kc in range(KVC):
                cast(wb_ukk[l][:, kc, :].rearrange("p (h c) -> p h c", c=128), ukv[:, kc, :, 0:128], ("w_ukv", l))
                cast(wb_ukv[l][:, kc, :].rearrange("p (h c) -> p h c", c=128), ukv[:, kc, :, 128:256], ("w_ukv", l))

        def cast_layer_rows(l):
            for g in range(DC):
                cast(wb_out[l, g], w_out[l][:, g * 128:(g + 1) * 128].rearrange("(kc p) n -> p kc n", p=128), ("w_out", l, g))
            for f in range(FC):
                cast(wb_gate[l, f], w_gate[l][:, f * 128:(f + 1) * 128].rearrange("(kc p) n -> p kc n", p=128), ("w_gate", l, f))
                cast(wb_up[l, f], w_up[l][:, f * 128:(f + 1) * 128].rearrange("(kc p) n -> p kc n", p=128), ("w_up", l, f))
            for oc in range(DC):
                for h in range(2):
                    cast(wb_down[l, oc, h], w_down[l][h * FH * 128:(h + 1) * FH * 128, oc * 128:(oc + 1) * 128].rearrange("(kc p) n -> p kc n", p=128), ("w_down", l, oc, h))
            for g in range(DC):
                cast(wb_ple[l, g], w_ple[l][:, g * 128:(g + 1) * 128].rearrange("(kc p) n -> p kc n", p=128), ("w_ple", l, g))
            for g in range(DC):
                cast(wb_pg[l, g], w_pg[l][:, g * 128:(g + 1) * 128].rearrange("(kc p) n -> p kc n", p=128), ("w_pg", l, g))

        def pipeline(items, PF):
            n = len(items)
            hs = {}
            for i in range(min(PF, n)):
                hs[i] = items[i][0]()
            for i in range(n):
                if i + PF < n:
                    hs[i + PF] = items[i + PF][0]()
                items[i][1](hs.pop(i))

        def ssq_rstd(ph, name, nfeat):
            pst, psb = psum[7]
            T = ph["T"]
            rst = ph[name]
            rsb = ph[name + "_b"]

            def acc(sq_ap, sq_buf, first, last):
                S.op("pe", lambda e: e.matmul(pst[:, 0:T], lhsT=ones_b, rhs=sq_ap, start=first, stop=last),
                     reads=[sq_buf, b_cb], writes=[psb], last=True)

            def fin():
                S.op("act", lambda e: e.activation(out=rst[:, 0:T], in_=pst[:, 0:T], func=AF.Sqrt, scale=1.0 / nfeat, bias=EPS),
                     reads=[psb], writes=[rsb])
                S.op("dve", lambda e: e.reciprocal(out=rst[:, 0:T], in_=rst[:, 0:T]), reads=[rsb], writes=[rsb])
            return acc, fin

        sqn = [0]

        def square_to(ph, in_ap, in_bufs, T):
            i = sqn[0] % 3
            sqn[0] += 1
            t, b = ph["sq"][i]
            S.op("act", lambda e: e.activation(out=t[:, 0:T], in_=in_ap, func=AF.Square), reads=in_bufs, writes=[b])
            return t[:, 0:T], b

        def setup_tables():
            with ExitStack() as ph:
                posi = sb(ph, "posi", [128, NT], I32)
                posf = sb(ph, "posf", [128, NT], F32)
                ang = sb(ph, "ang", [128, NT], F32)
                t1 = sb(ph, "tb1", [128, NT], F32)
                t2 = sb(ph, "tb2", [128, NT], F32)
                bp, bf_, ba_, b1, b2 = Buf(), Buf(), Buf(), Buf(), Buf()
                S.dma("sp", posi[:], posb[:, :], writes=[bp])
                S.op("dve", lambda e: e.tensor_copy(out=posf[:], in_=posi[:]), reads=[bp], writes=[bf_])
                MAGIC = 12582912.0
                TWO_PI = 2.0 * math.pi
                for ti, (ccol, shift) in enumerate([(c.c_invs, math.pi / 2), (c.c_invs, 0.0), (c.c_invm, math.pi / 2), (c.c_invm, 0.0)]):
                    S.op("dve", lambda e: e.tensor_scalar(out=ang[:], in0=posf[:], scalar1=cf[:, ccol:ccol + 1], scalar2=shift, op0=ALU.mult, op1=ALU.add),
                         reads=[bf_, b_cf], writes=[ba_])
                    S.op("dve", lambda e: e.tensor_scalar(out=t1[:], in0=ang[:], scalar1=1.0 / TWO_PI, scalar2=MAGIC, op0=ALU.mult, op1=ALU.add),
                         reads=[ba_], writes=[b1])
                    S.op("dve", lambda e: e.tensor_scalar(out=t1[:], in0=t1[:], scalar1=MAGIC, scalar2=None, op0=ALU.subtract),
                         reads=[b1], writes=[b1])
                    S.op("dve", lambda e: e.scalar_tensor_tensor(out=t2[:], in0=t1[:], scalar=-TWO_PI, in1=ang[:], op0=ALU.mult, op1=ALU.add),
                         reads=[b1, ba_], writes=[b2])
                    S.op("dve", lambda e: e.tensor_scalar(out=t2[:], in0=t2[:], scalar1=-3.1415925, scalar2=3.1415925, op0=ALU.max, op1=ALU.min),
                         reads=[b2], writes=[b2])
                    S.op("act", lambda e: e.activation(out=t2[:], in_=t2[:], func=AF.Sin), reads=[b2], writes=[b2])
                    S.dma("sp", tabs[ti], t2[:], reads=[b2], writes=[B("tabs")])
                S.barrier()

        def phase_p1(l, xcur, xck):
            with ExitStack() as ph_:
                T = T1
                ph = {"T": T}
                xt = sb(ph_, "p1x", [128, DC, T], F32)
                hT = sb(ph_, "p1h", [128, DC, T], BF16)
                ph["rs"] = sb(ph_, "p1rs", [128, T], F32)
                ph["rs_b"] = Buf()
                ph["sq"] = [(sb(ph_, "p1sq%d" % i, [128, T], BF16), Buf()) for i in range(3)]
                NWB = 4
                wts = [(sb(ph_, "p1w%d" % i, [128, DC, GW], BF16), Buf()) for i in range(NWB)]
                cs_t = sb(ph_, "p1cs", [128, 4, T], F32)
                b_cs = Buf()
                stg = [(sb(ph_, "p1st%d" % i, [128, T], F32), Buf()) for i in range(4)]
                stb = [(sb(ph_, "p1sb%d" % i, [128, 512], BF16), Buf()) for i in range(4)]
                zb = [(sb(ph_, "p1zb%d" % i, [128, T], BF16), Buf()) for i in range(2)]
                tmp = [(sb(ph_, "p1tm%d" % i, [128, T], F32), Buf()) for i in range(4)]
                b_x, b_h = Buf(), Buf()
                cnt = {"w": 0, "st": 0, "sb": 0, "zb": 0, "tm": 0}

                def rot(lst, k):
                    r = lst[cnt[k] % len(lst)]
                    cnt[k] += 1
                    return r

                for tt in range(NT // T):
                    tok = slice(tt * T, (tt + 1) * T)
                    for ti in range(4):
                        S.dma("sp", cs_t[:, ti, :], tabs[ti][:, tok], reads=[B("tabs")], writes=[b_cs])
                    CH = 8 if DC >= 8 else DC
                    for k0 in range(0, DC, CH):
                        S.dma("sp", xt[:, k0:k0 + CH, :], xcur[k0 * 128:(k0 + CH) * 128, tok].rearrange("(kc p) t -> p kc t", p=128),
                              reads=[B(xck)], writes=[b_x])
                    acc, fin = ssq_rstd(ph, "rs", D)
                    for kc in range(DC):
                        sq_ap, sq_b = square_to(ph, xt[:, kc, :], [b_x], T)
                        acc(sq_ap, sq_b, kc == 0, kc == DC - 1)
                    fin()
                    for kc in range(DC):
                        S.op("dve", lambda e: e.scalar_tensor_tensor(out=hT[:, kc, :], in0=xt[:, kc, :], scalar=vcol(l, "pre_mix", kc), in1=ph["rs"][:, 0:T],
                                                                    op0=ALU.mult, op1=ALU.mult), reads=[b_x, ph["rs_b"], b_vt], writes=[b_h])

                    dq_ = []

                    def fm_chunk(wt, wb_, col, epi):
                        pst, psb = PS()
                        for kc in range(DC):
                            S.op("pe", lambda e: e.matmul(pst[:, 0:T], lhsT=wt[:, kc, col:col + 128], rhs=hT[:, kc, :], start=(kc == 0), stop=(kc == DC - 1)),
                                 reads=[wb_, b_h], writes=[psb], last=(kc == DC - 1))
                        while dq_:
                            dq_.pop(0)()
                        epi(pst, psb)

                    def epi_copy(dst):
                        def f(pst, psb):
                            st, stb_ = rot(stg, "st")
                            S.op("act", lambda e: e.activation(out=st[:, 0:T], in_=pst[:, 0:T], func=AF.Copy), reads=[psb], writes=[stb_])
                            S.dma("sp", dst[:, tok], st[:, 0:T], reads=[stb_], writes=[B("scr")])
                        return f

                    def epi_gelu(dst):
                        def f(pst, psb):
                            st, stb_ = rot(stg, "st")
                            S.op("act", lambda e: e.activation(out=st[:, 0:T], in_=pst[:, 0:T], func=AF.Gelu_apprx_tanh), reads=[psb], writes=[stb_])
                            S.dma("sp", dst[:, tok], st[:, 0:T], reads=[stb_], writes=[B("scr")])
                        return f

                    def epi_rope(dst, mla):
                        ci, si = (2, 3) if mla else (0, 1)
                        rcol = c.c_rmla if mla else c.c_rswa

                        def f(pst, psb):
                            z, zb_ = rot(zb, "zb")
                            S.op("act", lambda e: e.activation(out=z[:, 0:T], in_=pst[:, 0:T], func=AF.Copy), reads=[psb], writes=[zb_])
                            t1, t1b = rot(tmp, "tm")
                            S.op("dve", lambda e: e.tensor_tensor(out=t1[:, 0:T], in0=pst[:, 0:T], in1=cs_t[:, ci, :], op=ALU.mult), reads=[psb, b_cs], writes=[t1b])

                            def second():
                                ps2, ps2b = psum[6]
                                S.op("pe", lambda e: e.matmul(ps2[:, 0:T], lhsT=cb[:, rcol:rcol + 128], rhs=z[:, 0:T], start=True, stop=True), reads=[zb_, b_cb], writes=[ps2b])
                                t2, t2b = rot(tmp, "tm")
                                S.op("dve", lambda e: e.tensor_tensor(out=t2[:, 0:T], in0=ps2[:, 0:T], in1=cs_t[:, si, :], op=ALU.mult), reads=[ps2b, b_cs], writes=[t2b])
                                o, ob = rot(stb, "sb")
                                S.op("dve", lambda e: e.tensor_tensor(out=o[:, 0:T], in0=t1[:, 0:T], in1=t2[:, 0:T], op=ALU.add), reads=[t1b, t2b], writes=[ob])
                                S.dma("sp", dst[:, tok], o[:, 0:T], reads=[ob], writes=[B("scr")])
                            dq_.append(second)
                        return f

                    def tm_group(wt, wb_, ncols, dst, dcol):
                        for ts in range(T // 128):
                            pst, psb = PS()
                            for kc in range(DC):
                                S.op("pe", lambda e: e.matmul(pst[:, 0:ncols], lhsT=hT[:, kc, ts * 128:(ts + 1) * 128], rhs=wt[:, kc, 0:ncols], start=(kc == 0), stop=(kc == DC - 1)),
                                     reads=[wb_, b_h], writes=[psb], last=(kc == DC - 1))
                            o, ob = rot(stb, "sb")
                            S.op("act", lambda e: e.activation(out=o[:, 0:ncols], in_=pst[:, 0:ncols], func=AF.Copy), reads=[psb], writes=[ob])
                            S.dma("sp", dst[tt * T + ts * 128: tt * T + (ts + 1) * 128, dcol:dcol + ncols], o[:, 0:ncols], reads=[ob], writes=[B("scr")])

                    def seg_of(col):
                        if col < c.o_ka: return ("rope", s_qa, (col - c.o_qa) // 128)
                        if col < c.o_va: return ("rope", s_ka, (col - c.o_ka) // 128)
                        if col < c.o_xr: return ("va", None, col - c.o_va)
                        if col < c.o_gr: return ("copy", s_xr, (col - c.o_xr) // 128)
                        if col < c.o_cq: return ("gelu", s_gr, (col - c.o_gr) // 128)
                        if col < c.o_ckv: return ("copy", s_cq, (col - c.o_cq) // 128)
                        return ("copy", s_ckv, (col - c.o_ckv) // 128)

                    items = []
                    for g in range(NG_IN):
                        def ld(g=g):
                            wt, wb_ = rot(wts, "w")
                            S.dma("sp", wt[:], wb_in[l, g], reads=[B("w_in", l, g)], writes=[wb_])
                            return wt, wb_

                        def cp(h, g=g):
                            wt, wb_ = h
                            kind = seg_of(g * GW)[0]
                            if kind == "va":
                                tm_group(wt, wb_, GW, s_va, g * GW - c.o_va)
                                return
                            for j in range(GW // 128):
                                kind, dst, idx = seg_of(g * GW + j * 128)
                                if kind == "rope":
                                    fm_chunk(wt, wb_, j * 128, epi_rope(dst[idx], False))
                                elif kind == "gelu":
                                    fm_chunk(wt, wb_, j * 128, epi_gelu(dst[idx]))
                                else:
                                    fm_chunk(wt, wb_, j * 128, epi_copy(dst[idx]))
                        items.append((ld, cp))

                    def ld_kr():
                        wt, wb_ = rot(wts, "w")
                        S.dma("sp", wt[:, :, 0:128], wb_kr[l], reads=[B("w_kr", l)], writes=[wb_])
                        return wt, wb_

                    def cp_kr(h):
                        fm_chunk(h[0], h[1], 0, epi_rope(s_kr, True))
                    items.append((ld_kr, cp_kr))
                    pipeline(items, 3)
                    while dq_:
                        dq_.pop(0)()
                S.barrier()

        def phase_swa(l):
            with ExitStack() as ph_:
                kt = sb(ph_, "swk", [128, NT], BF16)
                vtile = sb(ph_, "swv", [128, NB, 128], BF16)
                qts = [(sb(ph_, "swq%d" % i, [128, NT], BF16), Buf()) for i in range(2)]
                ets = [(sb(ph_, "swe%d" % i, [128, 512], BF16), Buf()) for i in range(4)]
                rcs = [(sb(ph_, "swr%d" % i, [128, 256], F32), Buf()) for i in range(2)]
                ots = [(sb(ph_, "swo%d" % i, [128, NT], F32), Buf()) for i in range(2)]
                b_k, b_v = Buf(), Buf()
                scale = 128.0 ** -0.5
                n = {"q": 0, "e": 0, "r": 0, "o": 0}
                for kvh in range(KV):
                    S.dma("sp", kt[:], s_ka[kvh], reads=[B("scr")], writes=[b_k])
                    S.dma("sp", vtile[:], s_va[:, kvh * 128:(kvh + 1) * 128].rearrange("(nb p) d -> p nb d", p=128), reads=[B("scr")], writes=[b_v])
                    for g in range(c.G):
                        h = kvh * c.G + g
                        qt, qb = qts[n["q"] % 2]; n["q"] += 1
                        S.dma("sp", qt[:], s_qa[h], reads=[B("scr")], writes=[qb])
                        ot, ob = ots[n["o"] % 2]; n["o"] += 1
                        def sw_scores(jp):
                            pst, psb = PS()
                            for u in range(2):
                                j = jp * 2 + u
                                jprev = max(j - 1, 0)
                                qs = qt[:, j * 128:(j + 1) * 128]
                                S.op("pe", lambda e: e.matmul(pst[:, u * 256:u * 256 + 128], lhsT=kt[:, jprev * 128:(jprev + 1) * 128], rhs=qs, start=True, stop=True),
                                     reads=[b_k, qb], writes=[psb], last=False)
                                S.op("pe", lambda e: e.matmul(pst[:, u * 256 + 128:u * 256 + 256], lhsT=kt[:, j * 128:(j + 1) * 128], rhs=qs, start=True, stop=True),
                                     reads=[b_k, qb], writes=[psb], last=(u == 1))
                            et, eb = ets[n["e"] % 4]; n["e"] += 1
                            S.op("act", lambda e: e.activation(out=et[:], in_=pst[:], func=AF.Exp, scale=scale), reads=[psb], writes=[eb])
                            mcol = c.c_mask0 if jp == 0 else c.c_mask
                            S.op("dve", lambda e: e.tensor_tensor(out=et[:], in0=et[:], in1=cb[:, mcol:mcol + 512], op=ALU.mult), reads=[eb, b_cb], writes=[eb])
                            return et, eb

                        def sw_rest(jp, et, eb):
                            pso, psob = PS()
                            psd, psdb = PS()
                            for u in range(2):
                                j = jp * 2 + u
                                jprev = max(j - 1, 0)
                                S.op("pe", lambda e: e.matmul(pso[:, u * 128:(u + 1) * 128], lhsT=vtile[:, jprev, :], rhs=et[:, u * 256:u * 256 + 128], start=True, stop=False),
                                     reads=[b_v, eb], writes=[psob], last=False)
                                S.op("pe", lambda e: e.matmul(pso[:, u * 128:(u + 1) * 128], lhsT=vtile[:, j, :], rhs=et[:, u * 256 + 128:u * 256 + 256], start=False, stop=True),
                                     reads=[b_v, eb], writes=[psob], last=False)
                            for u in range(2):
                                S.op("pe", lambda e: e.matmul(psd[:, u * 128:(u + 1) * 128], lhsT=ones_b, rhs=et[:, u * 256:u * 256 + 128], start=True, stop=False),
                                     reads=[b_cb, eb], writes=[psdb], last=False)
                                S.op("pe", lambda e: e.matmul(psd[:, u * 128:(u + 1) * 128], lhsT=ones_b, rhs=et[:, u * 256 + 128:u * 256 + 256], start=False, stop=True),
                                     reads=[b_cb, eb], writes=[psdb], last=(u == 1))
                            rc, rb = rcs[n["r"] % 2]; n["r"] += 1
                            S.op("dve", lambda e: e.tensor_scalar(out=rc[:], in0=psd[:, 0:256], scalar1=skt[:, l, h:h + 1], scalar2=None, op0=ALU.add), reads=[psdb, b_sk], writes=[rb])
                            S.op("dve", lambda e: e.reciprocal(out=rc[:], in_=rc[:]), reads=[rb], writes=[rb])
                            S.op("dve", lambda e: e.tensor_tensor(out=ot[:, jp * 256:(jp + 1) * 256], in0=pso[:, 0:256], in1=rc[:], op=ALU.mult), reads=[psob, rb], writes=[ob])

                        pend = [sw_scores(jp) for jp in range(min(2, NB // 2))]
                        for jp in range(NB // 2):
                            if jp + 2 < NB // 2:
                                pend.append(sw_scores(jp + 2))
                            sw_rest(jp, *pend.pop(0))
                        S.dma("sp", s_mix[h], ot[:], reads=[ob], writes=[B("mix")])
                S.barrier()

        def phase_rg(l):
            with ExitStack() as ph_:
                wa = sb(ph_, "rgwa", [128, RC, 128], BF16)
                wx = sb(ph_, "rgwx", [128, RC, 128], BF16)
                b_w = Buf()
                S.dma("sp", wa[:], wb_ga[l], reads=[B("w_g", l)], writes=[b_w])
                S.dma("sp", wx[:], wb_gx[l], reads=[B("w_g", l)], writes=[b_w])
                xp = [(sb(ph_, "rgx%d" % i, [128, NT + 3], F32), Buf()) for i in range(2)]
                gl = [(sb(ph_, "rgg%d" % i, [128, NT], F32), Buf()) for i in range(2)]
                names = ["xc", "r", "i", "a", "t"]
                tl = {nm: (sb(ph_, "rg_" + nm, [128, NT], F32), Buf()) for nm in names}
                xcb = (sb(ph_, "rg_xcb", [128, NT], BF16), Buf())
                for i in range(2):
                    S.op("dve", lambda e: e.memset(xp[i][0][:, 0:3], 0.0), writes=[xp[i][1]])
                for n_ in range(RC):
                    xpt, xpb = xp[n_ % 2]
                    gt, gb = gl[n_ % 2]
                    S.dma("sp", xpt[:, 3:3 + NT], s_xr[n_], reads=[B("scr")], writes=[xpb])
                    S.dma("sp", gt[:], s_gr[n_], reads=[B("scr")], writes=[gb])
                    xc, xcbuf = tl["xc"]
                    S.op("dve", lambda e: e.tensor_scalar(out=xc[:], in0=xpt[:, 3:3 + NT], scalar1=vcol(l, "cw3", n_), scalar2=vcol(l, "cb", n_), op0=ALU.mult, op1=ALU.add),
                         reads=[xpb, b_vt], writes=[xcbuf])
                    for j in range(3):
                        S.op("dve", lambda e: e.scalar_tensor_tensor(out=xc[:], in0=xpt[:, j:j + NT], scalar=vcol(l, "cw%d" % j, n_), in1=xc[:], op0=ALU.mult, op1=ALU.add),
                             reads=[xpb, b_vt, xcbuf], writes=[xcbuf])
                    S.op("act", lambda e: e.activation(out=xcb[0][:], in_=xc[:], func=AF.Copy), reads=[xcbuf], writes=[xcb[1]])
                    rt, rb = tl["r"]
                    it, ib = tl["i"]
                    for tt in range(NT // T1):
                        tok = slice(tt * T1, (tt + 1) * T1)
                        for (wt_, dst, dbuf, bname) in ((wa, rt, rb, "ba"), (wx, it, ib, "bx")):
                            pst, psb = PS()
                            S.op("pe", lambda e: e.matmul(pst[:, 0:T1], lhsT=wt_[:, n_, :], rhs=xcb[0][:, tok], start=True, stop=True), reads=[b_w, xcb[1]], writes=[psb])
                            S.op("act", lambda e: e.activation(out=dst[:, tok], in_=pst[:, 0:T1], func=AF.Sigmoid, bias=vcol(l, bname, n_)), reads=[psb, b_vt], writes=[dbuf])
                    at, ab = tl["a"]
                    tt_, tb = tl["t"]
                    S.op("act", lambda e: e.activation(out=at[:], in_=rt[:], func=AF.Exp, scale=c1t[:, l, n_:n_ + 1]), reads=[rb, b_c1], writes=[ab])
                    S.op("dve", lambda e: e.tensor_tensor(out=tt_[:], in0=at[:], in1=at[:], op=ALU.mult), reads=[ab], writes=[tb])
                    S.op("dve", lambda e: e.tensor_scalar(out=tt_[:], in0=tt_[:], scalar1=-1.0, scalar2=1.0, op0=ALU.mult, op1=ALU.add), reads=[tb], writes=[tb])
                    S.op("dve", lambda e: e.tensor_scalar(out=tt_[:], in0=tt_[:], scalar1=1e-20, scalar2=None, op0=ALU.max), reads=[tb], writes=[tb])
                    S.op("act", lambda e: e.activation(out=tt_[:], in_=tt_[:], func=AF.Sqrt), reads=[tb], writes=[tb])
                    S.op("dve", lambda e: e.tensor_tensor(out=tt_[:], in0=tt_[:], in1=it[:], op=ALU.mult), reads=[tb, ib], writes=[tb])
                    S.op("dve", lambda e: e.tensor_tensor(out=tt_[:], in0=tt_[:], in1=xc[:], op=ALU.mult), reads=[tb, xcbuf], writes=[tb])
                    S.op("dve", lambda e: e.tensor_tensor_scan(out=rt[:], data0=at[:], data1=tt_[:], initial=0.0, op0=ALU.mult, op1=ALU.add), reads=[ab, tb, rb], writes=[rb])
                    S.op("dve", lambda e: e.tensor_tensor(out=rt[:], in0=rt[:], in1=gt[:], op=ALU.mult), reads=[rb, gb], writes=[rb])
                    S.dma("sp", s_mix[HS + n_], rt[:], reads=[rb], writes=[B("mix")])
                S.barrier()

        def phase_mla_proj(l):
            with ExitStack() as ph_:
                T = T1
                ph = {"T": T}
                ph["rs"] = sb(ph_, "m1rs", [128, T], F32)
                ph["rs_b"] = Buf()
                ph["sq"] = [(sb(ph_, "m1sq%d" % i, [128, T], BF16), Buf()) for i in range(3)]
                wqn = sb(ph_, "m1wqn", [128, QC, MH * 128], BF16)
                wqr = sb(ph_, "m1wqr", [128, QC, MH * 64], BF16)
                wkk = sb(ph_, "m1wkk", [128, KVC, MH * 128], BF16)
                wkv = sb(ph_, "m1wkv", [128, KVC, MH * 128], BF16)
                b_w = Buf()
                S.dma("sp", wqn[:], wb_uqn[l], reads=[B("w_uq", l)], writes=[b_w])
                S.dma("sp", wqr[:], wb_uqr[l], reads=[B("w_uq", l)], writes=[b_w])
                S.dma("sp", wkk[:], wb_ukk[l], reads=[B("w_ukv", l)], writes=[b_w])
                S.dma("sp", wkv[:], wb_ukv[l], reads=[B("w_ukv", l)], writes=[b_w])
                cin = sb(ph_, "m1in", [128, QC, T], F32)
                cn = sb(ph_, "m1cn", [128, QC, T], BF16)
                b_in, b_cn = Buf(), Buf()
                cs_t = sb(ph_, "m1cs", [128, 2, T], F32)
                b_cs = Buf()
                stb = [(sb(ph_, "m1sb%d" % i, [128, 512], BF16), Buf()) for i in range(4)]
                zb = [(sb(ph_, "m1zb%d" % i, [128, T], BF16), Buf()) for i in range(2)]
                tmp = [(sb(ph_, "m1tm%d" % i, [128, T], F32), Buf()) for i in range(4)]
                cnt = {"sb": 0, "zb": 0, "tm": 0}

                def rot(lst, k):
                    r = lst[cnt[k] % len(lst)]
                    cnt[k] += 1
                    return r

                for tt in range(NT // T):
                    tok = slice(tt * T, (tt + 1) * T)
                    for ti in range(2):
                        S.dma("sp", cs_t[:, ti, :], tabs[2 + ti][:, tok], reads=[B("tabs")], writes=[b_cs])
                    for (src, nch, gname, nfeat, side) in ((s_cq, QC, "qn", c.QR, "q"), (s_ckv, KVC, "kvn", c.KVR, "kv")):
                        S.dma("sp", cin[:, 0:nch, :], src[:, :, tok].rearrange("kc p t -> p kc t"), reads=[B("scr")], writes=[b_in])
                        acc, fin = ssq_rstd(ph, "rs", nfeat)
                        for kc in range(nch):
                            sq_ap, sq_b = square_to(ph, cin[:, kc, :], [b_in], T)
                            acc(sq_ap, sq_b, kc == 0, kc == nch - 1)
                        fin()
                        for kc in range(nch):
                            S.op("dve", lambda e: e.scalar_tensor_tensor(out=cn[:, kc, :], in0=cin[:, kc, :], scalar=vcol(l, gname, kc), in1=ph["rs"][:, 0:T],
                                                                        op0=ALU.mult, op1=ALU.mult), reads=[b_in, ph["rs_b"], b_vt], writes=[b_cn])

                        def fm(wt, col, nch=nch):
                            pst, psb = PS()
                            for kc in range(nch):
                                S.op("pe", lambda e: e.matmul(pst[:, 0:T], lhsT=wt[:, kc, col:col + 128], rhs=cn[:, kc, :], start=(kc == 0), stop=(kc == nch - 1)),
                                     reads=[b_w, b_cn], writes=[psb], last=(kc == nch - 1))
                            return pst, psb

                        def store_bf(pst, psb, dst):
                            o, ob = rot(stb, "sb")
                            S.op("act", lambda e: e.activation(out=o[:, 0:T], in_=pst[:, 0:T], func=AF.Copy), reads=[psb], writes=[ob])
                            S.dma("sp", dst[:, tok], o[:, 0:T], reads=[ob], writes=[B("scr2")])

                        if side == "q":
                            for h in range(MH):
                                pst, psb = fm(wqn, h * 128)
                                store_bf(pst, psb, s_qn[h])
                            for cc in range(MH // 2):
                                pst, psb = fm(wqr, cc * 128)
                                z, zb_ = rot(zb, "zb")
                                S.op("act", lambda e: e.activation(out=z[:, 0:T], in_=pst[:, 0:T], func=AF.Copy), reads=[psb], writes=[zb_])
                                t1, t1b = rot(tmp, "tm")
                                S.op("dve", lambda e: e.tensor_tensor(out=t1[:, 0:T], in0=pst[:, 0:T], in1=cs_t[:, 0, :], op=ALU.mult), reads=[psb, b_cs], writes=[t1b])
                                ps2, ps2b = PS()
                                S.op("pe", lambda e: e.matmul(ps2[:, 0:T], lhsT=cb[:, c.c_rmla:c.c_rmla + 128], rhs=z[:, 0:T], start=True, stop=True), reads=[zb_, b_cb], writes=[ps2b])
                                t2, t2b = rot(tmp, "tm")
                                S.op("dve", lambda e: e.tensor_tensor(out=t2[:, 0:T], in0=ps2[:, 0:T], in1=cs_t[:, 1, :], op=ALU.mult), reads=[ps2b, b_cs], writes=[t2b])
                                o, ob = rot(stb, "sb")
                                S.op("dve", lambda e: e.tensor_tensor(out=o[:, 0:T], in0=t1[:, 0:T], in1=t2[:, 0:T], op=ALU.add), reads=[t1b, t2b], writes=[ob])
                                S.dma("sp", s_qrp[cc][:, tok], o[:, 0:T], reads=[ob], writes=[B("scr2")])
                        else:
                            for h in range(MH):
                                pst, psb = fm(wkk, h * 128)
                                store_bf(pst, psb, s_kn[h])
                            VG = min(512, MH * 128)
                            for ts in range(T // 128):
                                for v0 in range(0, MH * 128, VG):
                                    pst, psb = PS()
                                    for kc in range(nch):
                                        S.op("pe", lambda e: e.matmul(pst[:, 0:VG], lhsT=cn[:, kc, ts * 128:(ts + 1) * 128], rhs=wkv[:, kc, v0:v0 + VG], start=(kc == 0), stop=(kc == nch - 1)),
                                             reads=[b_w, b_cn], writes=[psb], last=(kc == nch - 1))
                                    o, ob = rot(stb, "sb")
                                    S.op("act", lambda e: e.activation(out=o[:, 0:VG], in_=pst[:, 0:VG], func=AF.Copy), reads=[psb], writes=[ob])
                                    S.dma("sp", s_vc[tt * T + ts * 128:tt * T + (ts + 1) * 128, v0:v0 + VG], o[:, 0:VG], reads=[ob], writes=[B("scr2")])
                S.barrier()

        def phase_mla_attn(l):
            with ExitStack() as ph_:
                T = T1
                krt = sb(ph_, "makr", [128, NT], BF16)
                b_kr = Buf()
                S.dma("sp", krt[:], s_kr[:, :], reads=[B("scr")], writes=[b_kr])
                qn_t = [(sb(ph_, "maqn%d" % i, [128, NT], BF16), Buf()) for i in range(2)]
                qr_t = [(sb(ph_, "maqr%d" % i, [128, NT], BF16), Buf()) for i in range(2)]
                kn_t = [(sb(ph_, "makn%d" % i, [128, NT], BF16), Buf()) for i in range(2)]
                v_t = [(sb(ph_, "mav%d" % i, [128, NB, 128], BF16), Buf()) for i in range(2)]
                ets = [(sb(ph_, "mae%d" % i, [128, 512], BF16), Buf()) for i in range(4)]
                rcs = [(sb(ph_, "mar%d" % i, [128, 512], F32), Buf()) for i in range(2)]
                ots = [(sb(ph_, "mao%d" % i, [128, 512], F32), Buf()) for i in range(2)]
                scale = 192.0 ** -0.5
                ne = [0, 0, 0, 0]
                for h in range(MH):
                    qn, qnb = qn_t[h % 2]
                    kn, knb = kn_t[h % 2]
                    vv, vb = v_t[h % 2]
                    S.dma("sp", qn[:], s_qn[h], reads=[B("scr2")], writes=[qnb])
                    S.dma("sp", kn[:], s_kn[h], reads=[B("scr2")], writes=[knb])
                    S.dma("sp", vv[:], s_vc[:, h * 128:(h + 1) * 128].rearrange("(nb p) d -> p nb d", p=128), reads=[B("scr2")], writes=[vb])
                    if h % 2 == 0:
                        qr, qrb = qr_t[(h // 2) % 2]
                        S.dma("sp", qr[:], s_qrp[h // 2], reads=[B("scr2")], writes=[qrb])
                    hp = (h % 2) * 64
                    for qt in range(NT // T):
                        q0 = qt * T
                        nkb = (q0 + T) // 128
                        pso, psob = psum[4 + 2 * (ne[3] % 2)]
                        psd, psdb = psum[5 + 2 * (ne[3] % 2)]
                        ne[3] += 1
                        def scores(kb):
                            lo = max(0, kb * 128 - q0)
                            diag = kb * 128 >= q0
                            pst, psb = PS(0, 4)
                            ks = slice(kb * 128, (kb + 1) * 128)
                            S.op("pe", lambda e: e.matmul(pst[:, lo:T], lhsT=kn[:, ks], rhs=qn[:, q0 + lo:q0 + T], start=True, stop=False),
                                 reads=[knb, qnb], writes=[psb], last=False)
                            S.op("pe", lambda e: e.matmul(pst[:, lo:T], lhsT=krt[hp:hp + 64, ks], rhs=qr[hp:hp + 64, q0 + lo:q0 + T], start=False, stop=(not diag)),
                                 reads=[b_kr, qrb], writes=[psb], last=(not diag))
                            if diag:
                                S.op("pe", lambda e: e.matmul(pst[:, lo:lo + 128], lhsT=id_b, rhs=cb[:, c.c_negtri:c.c_negtri + 128], start=False, stop=True),
                                     reads=[b_cb], writes=[psb], last=True)
                            return pst, psb, lo

                        def rest(kb, pst, psb, lo):
                            et, eb = ets[ne[0] % 4]; ne[0] += 1
                            S.op("act", lambda e: e.activation(out=et[:, lo:T], in_=pst[:, lo:T], func=AF.Exp, scale=scale), reads=[psb], writes=[eb])
                            S.op("pe", lambda e: e.matmul(pso[:, lo:T], lhsT=vv[:, kb, :], rhs=et[:, lo:T], start=(kb == 0), stop=(kb == nkb - 1)),
                                 reads=[vb, eb], writes=[psob], last=False)
                            S.op("pe", lambda e: e.matmul(psd[:, lo:T], lhsT=ones_b, rhs=et[:, lo:T], start=(kb == 0), stop=(kb == nkb - 1)),
                                 reads=[b_cb, eb], writes=[psdb], last=True)

                        pend = [scores(kb) for kb in range(min(2, nkb))]
                        for kb in range(nkb):
                            if kb + 2 < nkb:
                                pend.append(scores(kb + 2))
                            rest(kb, *pend.pop(0))
                        rc, rb = rcs[ne[1] % 2]; ne[1] += 1
                        ot, ob = ots[ne[2] % 2]; ne[2] += 1
                        S.op("dve", lambda e: e.reciprocal(out=rc[:, 0:T], in_=psd[:, 0:T]), reads=[psdb], writes=[rb])
                        S.op("dve", lambda e: e.tensor_tensor(out=ot[:, 0:T], in0=pso[:, 0:T], in1=rc[:, 0:T], op=ALU.mult), reads=[psob, rb], writes=[ob])
                        S.dma("sp", s_mix[HS + RC + h][:, q0:q0 + T], ot[:, 0:T], reads=[ob], writes=[B("mix")])
                S.barrier()

        def phase_rows(l, xcur, xcur_key, xnext, xnext_key):
            with ExitStack() as ph_:
                T = T2
                ph = {"T": T}
                ph["rs"] = sb(ph_, "r5rs", [128, T], F32)
                ph["rs_b"] = Buf()
                ph["rs2"] = sb(ph_, "r5rs2", [128, T], F32)
                ph["rs2_b"] = Buf()
                ph["sq"] = [(sb(ph_, "r5sq%d" % i, [128, T], BF16), Buf()) for i in range(3)]
                yt = sb(ph_, "r5y", [128, DC, T], F32)
                hT = sb(ph_, "r5h", [128, DC, T], BF16)
                aT = sb(ph_, "r5a", [128, FH, T], BF16)
                pbb = sb(ph_, "r5pb", [128, PC, T], BF16)
                b_h, b_a, b_pb = Buf(), Buf(), Buf()
                by = [Buf() for _ in range(DC)]
                dq_ = []
                WSZ = max(FH * 128, DC * 128, PC * 128)
                NWB = 3
                wts = [(sb(ph_, "r5w%d" % i, [128, WSZ], BF16), Buf()) for i in range(NWB)]
                tmp = [(sb(ph_, "r5tm%d" % i, [128, T], F32), Buf()) for i in range(3)]
                xin = [(sb(ph_, "r5xi%d" % i, [128, T], F32), Buf()) for i in range(4)]
                cnt = {"w": 0, "tm": 0, "xi": 0}

                def rot(lst, k):
                    r = lst[cnt[k] % len(lst)]
                    cnt[k] += 1
                    return r

                def ldw(src, key, ncol):
                    def f():
                        wt, wb_ = rot(wts, "w")
                        S.dma("sp", wt[:, 0:ncol], src.rearrange("p k n -> p (k n)"), reads=[B(*key)], writes=[wb_])
                        return wt, wb_
                    return f

                for tt in range(NT // T):
                    tok = slice(tt * T, (tt + 1) * T)

                    def xchunk(ap, oc):
                        return ap[oc * 128:(oc + 1) * 128, tok]

                    def resid(gname, xsrc, ksrc, xdst, kdst, next_sq=False, to_bf=False):
                        if next_sq:
                            acc, fin = ssq_rstd(ph, "rs2", D)
                        xq = []

                        def xload(o_):
                            xi_, xib_ = rot(xin, "xi")
                            S.dma("sp", xi_[:, 0:T], xchunk(xsrc, o_), reads=[B(ksrc)], writes=[xib_])
                            xq.append((xi_, xib_))
                        for o_ in range(min(2, DC)):
                            xload(o_)
                        for oc in range(DC):
                            t, tb = rot(tmp, "tm")
                            S.op("dve", lambda e: e.scalar_tensor_tensor(out=t[:, 0:T], in0=yt[:, oc, :], scalar=vcol(l, gname, oc), in1=ph["rs"][:, 0:T], op0=ALU.mult, op1=ALU.mult),
                                 reads=[by[oc], ph["rs_b"], b_vt], writes=[tb])
                            xi, xib = xq.pop(0)
                            if oc + 2 < DC:
                                xload(oc + 2)
                            S.op("dve", lambda e: e.tensor_tensor(out=yt[:, oc, :], in0=xi[:, 0:T], in1=t[:, 0:T], op=ALU.add), reads=[tb, xib, by[oc]], writes=[by[oc]])
                            S.dma("sp", xchunk(xdst, oc), yt[:, oc, :], reads=[by[oc]], writes=[B(kdst)])
                            if next_sq:
                                sq_ap, sq_b = square_to(ph, yt[:, oc, :], [by[oc]], T)
                                acc(sq_ap, sq_b, oc == 0, oc == DC - 1)
                            if to_bf:
                                S.op("act", lambda e: e.activation(out=hT[:, oc, :], in_=yt[:, oc, :], func=AF.Copy), reads=[by[oc]], writes=[b_h])
                        if next_sq:
                            fin()

                    CH = 8 if DC >= 8 else DC

                    def load_mix(k0, tk):
                        S.dma("sp", yt[:, k0:k0 + CH, :], s_mix[k0:k0 + CH, :, tk].rearrange("kc p t -> p kc t"), reads=[B("mix")], writes=by[k0:k0 + CH])

                    def load_p(tk):
                        for kc in range(PC):
                            xi_, xib_ = rot(xin, "xi")
                            S.dma("sp", xi_[:, 0:T], pT[l][kc * 128:(kc + 1) * 128, tk], writes=[xib_])
                            S.op("dve", lambda e: e.tensor_copy(out=pbb[:, kc, :], in_=xi_[:, 0:T]), reads=[xib_], writes=[b_pb])
                    if tt == 0:
                        for k0 in range(0, DC, CH):
                            load_mix(k0, tok)
                        load_p(tok)
                    has_next = tt + 1 < NT // T
                    tokn = slice((tt + 1) * T, (tt + 2) * T)
                    for gi, (c0, c1_) in enumerate(((0, HS), (HS, HS + RC), (HS + RC, DC))):
                        acc, fin = ssq_rstd(ph, "rs", (c1_ - c0) * 128)
                        for kc in range(c0, c1_):
                            sq_ap, sq_b = square_to(ph, yt[:, kc, :], [by[kc]], T)
                            acc(sq_ap, sq_b, kc == c0, kc == c1_ - 1)
                        fin()
                        for kc in range(c0, c1_):
                            S.op("dve", lambda e: e.scalar_tensor_tensor(out=hT[:, kc, :], in0=yt[:, kc, :], scalar=vcol(l, "gn", kc), in1=ph["rs"][:, 0:T], op0=ALU.mult, op1=ALU.mult),
                                 reads=[by[kc], ph["rs_b"], b_vt], writes=[b_h])

                    def dense_to_y(wb_t, key, src_bf, src_buf, nkc):
                        acc, fin = ssq_rstd(ph, "rs", D)
                        items = []
                        for oc in range(DC):
                            def cp(hd, oc=oc):
                                wt, wb_ = hd
                                pst, psb = PS()
                                for kc in range(nkc):
                                    S.op("pe", lambda e: e.matmul(pst[:, 0:T], lhsT=wt[:, kc * 128:(kc + 1) * 128], rhs=src_bf[:, kc, :], start=(kc == 0), stop=(kc == nkc - 1)),
                                         reads=[wb_, src_buf], writes=[psb], last=(kc == nkc - 1))
                                while dq_:
                                    dq_.pop(0)()
                                S.op("dve", lambda e: e.tensor_copy(out=yt[:, oc, :], in_=pst[:, 0:T]), reads=[psb], writes=[by[oc]])
                                sq_ap, sq_b = square_to(ph, pst[:, 0:T], [psb], T)
                                dq_.append(lambda: acc(sq_ap, sq_b, oc == 0, oc == DC - 1))
                            items.append((ldw(wb_t[l, oc], key + (l, oc), nkc * 128), cp))
                        pipeline(items, 2)
                        while dq_:
                            dq_.pop(0)()
                        fin()

                    dense_to_y(wb_out, ("w_out",), hT, b_h, DC)
                    resid("post_mix", xcur, xcur_key, xs1, "xs1", next_sq=True)
                    for kc in range(DC):
                        S.op("dve", lambda e: e.scalar_tensor_tensor(out=hT[:, kc, :], in0=yt[:, kc, :], scalar=vcol(l, "pre_ffn", kc), in1=ph["rs2"][:, 0:T], op0=ALU.mult, op1=ALU.mult),
                             reads=[by[kc], ph["rs2_b"], b_vt], writes=[b_h])
                    for hh in range(2):
                        items = []
                        for k in range(FH):
                            f = hh * FH + k
                            for (wsrc, key) in ((wb_gate, "w_gate"), (wb_up, "w_up")):
                                def cp(hd, k=k, key=key):
                                    wt, wb_ = hd
                                    pst, psb = PS()
                                    for kc in range(DC):
                                        S.op("pe", lambda e: e.matmul(pst[:, 0:T], lhsT=wt[:, kc * 128:(kc + 1) * 128], rhs=hT[:, kc, :], start=(kc == 0), stop=(kc == DC - 1)),
                                             reads=[wb_, b_h], writes=[psb], last=(kc == DC - 1))
                                    if key == "w_gate":
                                        t, tb = rot(tmp, "tm")
                                        S.op("act", lambda e: e.activation(out=t[:, 0:T], in_=pst[:, 0:T], func=AF.Silu), reads=[psb], writes=[tb])
                                        ph["gate"] = (t, tb)
                                    else:
                                        t, tb = ph["gate"]
                                        S.op("dve", lambda e: e.tensor_tensor(out=aT[:, k, :], in0=pst[:, 0:T], in1=t[:, 0:T], op=ALU.mult), reads=[psb, tb], writes=[b_a])
                                items.append((ldw(wsrc[l, f], (key, l, f), DC * 128), cp))
                        pipeline(items, 2)
                        if hh == 1:
                            acc, fin = ssq_rstd(ph, "rs", D)
                        items = []
                        for oc in range(DC):
                            def cp(hd, oc=oc, hh=hh):
                                wt, wb_ = hd
                                pst, psb = PS()
                                for k in range(FH):
                                    S.op("pe", lambda e: e.matmul(pst[:, 0:T], lhsT=wt[:, k * 128:(k + 1) * 128], rhs=aT[:, k, :], start=(k == 0), stop=(k == FH - 1)),
                                         reads=[wb_, b_a], writes=[psb], last=(k == FH - 1))
                                while dq_:
                                    dq_.pop(0)()
                                if hh == 0:
                                    S.op("dve", lambda e: e.tensor_copy(out=yt[:, oc, :], in_=pst[:, 0:T]), reads=[psb], writes=[by[oc]])
                                else:
                                    S.op("dve", lambda e: e.tensor_tensor(out=yt[:, oc, :], in0=pst[:, 0:T], in1=yt[:, oc, :], op=ALU.add), reads=[psb, by[oc]], writes=[by[oc]])
                                    sq_ap, sq_b = square_to(ph, yt[:, oc, :], [by[oc]], T)
                                    dq_.append(lambda: acc(sq_ap, sq_b, oc == 0, oc == DC - 1))
                            items.append((ldw(wb_down[l, oc, hh], ("w_down", l, oc, hh), FH * 128), cp))
                        pipeline(items, 2)
                    while dq_:
                        dq_.pop(0)()
                    fin()
                    resid("post_ffn", xs1, "xs1", xs2, "xs2", to_bf=True)
                    dense_to_y(wb_ple, ("w_ple",), pbb, b_pb, PC)
                    for oc in range(DC):
                        S.op("dve", lambda e: e.scalar_tensor_tensor(out=yt[:, oc, :], in0=yt[:, oc, :], scalar=vcol(l, "ple_n", oc), in1=ph["rs"][:, 0:T], op0=ALU.mult, op1=ALU.mult),
                             reads=[by[oc], ph["rs_b"], b_vt], writes=[by[oc]])
                    items = []
                    for oc in range(DC):
                        def cp(hd, oc=oc):
                            wt, wb_ = hd
                            pst, psb = PS()
                            for kc in range(DC):
                                S.op("pe", lambda e: e.matmul(pst[:, 0:T], lhsT=wt[:, kc * 128:(kc + 1) * 128], rhs=hT[:, kc, :], start=(kc == 0), stop=(kc == DC - 1)),
                                     reads=[wb_, b_h], writes=[psb], last=(kc == DC - 1))
                            t, tb = rot(tmp, "tm")
                            S.op("act", lambda e: e.activation(out=t[:, 0:T], in_=pst[:, 0:T], func=AF.Sigmoid, bias=vcol(l, "bpg", oc)), reads=[psb, b_vt], writes=[tb])
                            S.op("dve", lambda e: e.tensor_tensor(out=t[:, 0:T], in0=t[:, 0:T], in1=yt[:, oc, :], op=ALU.mult), reads=[tb, by[oc]], writes=[tb])
                            xi, xib = rot(xin, "xi")
                            S.dma("sp", xi[:, 0:T], xchunk(xs2, oc), reads=[B("xs2")], writes=[xib])
                            S.op("dve", lambda e: e.tensor_tensor(out=xi[:, 0:T], in0=xi[:, 0:T], in1=t[:, 0:T], op=ALU.add), reads=[tb, xib], writes=[xib])
                            S.dma("sp", xchunk(xnext, oc), xi[:, 0:T], reads=[xib], writes=[B(xnext_key)])
                            if has_next and (oc + 1) % CH == 0:
                                load_mix(oc + 1 - CH, tokn)
                            if has_next and oc == DC - 1:
                                load_p(tokn)
                        items.append((ldw(wb_pg[l, oc], ("w_pg", l, oc), DC * 128), cp))
                    pipeline(items, 2)
                S.barrier()

        def on(p):
            return phases is None or p in phases
        cast_layer_mixer(0)
        cast_layer_rows(0)
        if on("tabs"):
            setup_tables()
        for l in range(L):
            xcur, xck = (xT, "x_in") if l == 0 else (xres, "xres")
            xnext, xnk = (yT, "y_out") if l == L - 1 else (xres, "xres")
            if on("p1"):
                phase_p1(l, xcur, xck)
            if l + 1 < L:
                cast_layer_mixer(l + 1)
            if on("swa"):
                phase_swa(l)
            if on("rg"):
                phase_rg(l)
            if on("mlap"):
                phase_mla_proj(l)
            if on("mlaa"):
                phase_mla_attn(l)
            if on("rows"):
                phase_rows(l, xcur, xck, xnext, xnk)
            if l + 1 < L:
                cast_layer_rows(l + 1)
            if phases is not None and "l0" in phases:
                break
        S.barrier(final=True)
    return nc


def _consts(cfg):
    c = cfg
    cf = np.zeros((128, c.CW), np.float32)
    cf[:, c.c_ones:c.c_ones + 128] = 1.0
    cf[:, c.c_id:c.c_id + 128] = np.eye(128, dtype=np.float32)
    R = np.zeros((128, 128), np.float32)
    for m in range(128):
        if m < 64: R[m, m + 64] = -1.0
        else: R[m, m - 64] = 1.0
    cf[:, c.c_rswa:c.c_rswa + 128] = R.T
    R = np.zeros((128, 128), np.float32)
    for m in range(128):
        if (m % 64) < 32: R[m, m + 32] = -1.0
        else: R[m, m - 32] = 1.0
    cf[:, c.c_rmla:c.c_rmla + 128] = R.T
    k = np.arange(128)[:, None]
    q = np.arange(128)[None, :]
    prev = (k > q).astype(np.float32)
    cur = (k <= q).astype(np.float32)
    m = np.concatenate([prev, cur, prev, cur], axis=1)
    cf[:, c.c_mask:c.c_mask + 512] = m
    m0 = m.copy()
    m0[:, 0:128] = 0.0
    cf[:, c.c_mask0:c.c_mask0 + 512] = m0
    cf[:, c.c_negtri:c.c_negtri + 128] = np.where(k <= q, 0.0, -30000.0).astype(np.float32)
    p = np.arange(128)
    cf[:, c.c_invs] = (THETA ** (-(2.0 * (p % 64)) / 128.0)).astype(np.float32)
    cf[:, c.c_invm] = (THETA ** (-(2.0 * (p % 32)) / 64.0)).astype(np.float32)
    return cf


def _pack_vecs(cfg, inp, l):
    c = cfg
    v = np.zeros((128, c.NV), np.float32)

    def put(name, arr):
        arr = np.asarray(arr, np.float32).reshape(-1, 128).T
        v[:, c.voff[name]:c.voff[name] + arr.shape[1]] = arr
    put("pre_mix", inp["pre_mix_norm"][l])
    for j in range(4):
        put("cw%d" % j, inp["rg_conv_w"][l, j])
    put("cb", inp["rg_conv_b"][l])
    put("ba", inp["rg_gate_a_b"][l])
    put("bx", inp["rg_gate_x_b"][l])
    put("lam", inp["rg_lambda"][l])
    put("qn", inp["mla_q_norm"][l])
    put("kvn", inp["mla_kv_norm"][l])
    put("gn", inp["group_norm"][l])
    put("post_mix", inp["post_mix_norm"][l])
    put("pre_ffn", inp["pre_ffn_norm"][l])
    put("post_ffn", inp["post_ffn_norm"][l])
    put("ple_n", inp["ple_norm"][l])
    put("bpg", inp["b_ple_gate"][l])
    return v


def make_in_maps(cfg, inp, ncores):
    c = cfg
    L = c.DEPTH
    f = lambda a: np.ascontiguousarray(np.asarray(a, np.float32))
    shared = {
        "vecs": np.stack([_pack_vecs(c, inp, l) for l in range(L)]),
        "sinkb": np.ascontiguousarray(np.broadcast_to(np.asarray(inp["swa_sinks"], np.float32)[:, None, :], (L, 128, c.HS))),
        "consts": _consts(c),
        "w_in": f(inp["w_in"]), "w_ga": f(inp["rg_gate_a_w"]), "w_gx": f(inp["rg_gate_x_w"]),
        "w_uq": f(inp["mla_w_uq"]), "w_ukv": f(inp["mla_w_ukv"]), "w_out": f(inp["w_out"]),
        "w_gate": f(inp["w_gate"]), "w_up": f(inp["w_up"]), "w_down": f(inp["w_down"]),
        "w_ple": f(inp["w_ple"]), "w_pg": f(inp["w_ple_gate"]),
    }
    x = np.asarray(inp["x"], np.float32)
    p = np.asarray(inp["p"], np.float32)
    pos = np.asarray(inp["positions"]).astype(np.int32)
    maps = []
    for b in range(ncores):
        m = dict(shared)
        m["xT"] = np.ascontiguousarray(x[b].T)
        m["pT"] = np.ascontiguousarray(np.transpose(p[:, b], (0, 2, 1)))
        m["posb"] = np.ascontiguousarray(np.broadcast_to(pos[b][None, :], (128, c.NT)))
        maps.append(m)
    return maps


def kernel(**inputs):
    cfg = Cfg()
    nb = np.asarray(inputs["x"]).shape[0]
    nc = build(cfg)
    maps = make_in_maps(cfg, inputs, nb)
    res = run_bass_kernel_spmd(nc, maps, core_ids=list(range(nb)))
    out = np.stack([np.ascontiguousarray(r["yT"].T) for r in res.results]).astype(np.float32)
    return out
```
